# Optimizing a Trainium2 kernel written in Bass

```python
import math
import jax, jax.numpy as jnp
from jax import lax
import numpy as np

D_MODEL = 2048
BATCH = 4
SEQ = 2048
DEPTH = 4
DEC_BATCH = 32
DEC_SEQ = 1
PAST_LEN = 16384
PAGE_SIZE = 128

N_A_LAYERS = DEPTH // 2
N_B_LAYERS = DEPTH - N_A_LAYERS
D_POOL = D_MODEL
POOL_WINDOWS = (2, 4, 8, 16)
N_POOL_GROUPS = len(POOL_WINDOWS)
POOL_GROUP = D_POOL // N_POOL_GROUPS
POOL_STATE = max(POOL_WINDOWS) - 1
HEAD_DIM = 64
N_HEADS = D_MODEL // HEAD_DIM
N_KV_HEADS = N_HEADS // 8
GROUP = N_HEADS // N_KV_HEADS
D_ATTN = N_HEADS * HEAD_DIM
KV_DIM = N_KV_HEADS * HEAD_DIM
WINDOW = 128
ROT_DIM = HEAD_DIM // 4
ROPE_THETA = 500000.0
ALPHA = (2 * DEPTH) ** 0.25
BETA = (8 * DEPTH) ** -0.25
LN_EPS = 1e-5
NEG = -1e30

kernel_name = "yoco_pool_swa_sink_decoder_step"


def layer_norm(x, g, b):
    xf = x.astype(jnp.float32)
    mu = jnp.mean(xf, axis=-1, keepdims=True)
    var = jnp.mean(jnp.square(xf - mu), axis=-1, keepdims=True)
    return ((xf - mu) * lax.rsqrt(var + LN_EPS) * g.astype(jnp.float32) + b.astype(jnp.float32)).astype(x.dtype)


def rope_partial(x, pos):
    half = ROT_DIM // 2
    inv_freq = ROPE_THETA ** (-jnp.arange(0, ROT_DIM, 2, dtype=jnp.float32) / ROT_DIM)
    ang = pos.astype(jnp.float32)[:, None] * inv_freq[None, :]
    cos = jnp.cos(ang)[:, None, :]
    sin = jnp.sin(ang)[:, None, :]
    xf = x.astype(jnp.float32)
    x1, x2, rest = xf[..., :half], xf[..., half:ROT_DIM], xf[..., ROT_DIM:]
    out = jnp.concatenate([x1 * cos - x2 * sin, x2 * cos + x1 * sin, rest], axis=-1)
    return out.astype(x.dtype)


def multiscale_pool(u, prefix, pos):
    T = u.shape[1]
    P = POOL_STATE
    ext = jnp.concatenate([prefix.astype(u.dtype), u], axis=1)
    extf = ext.astype(jnp.float32)
    c = jnp.cumsum(extf, axis=1)
    c = jnp.concatenate([jnp.zeros_like(c[:, :1]), c], axis=1)
    hi = c[:, P + 1:P + 1 + T]
    outs = []
    for g, w in enumerate(POOL_WINDOWS):
        sl = slice(g * POOL_GROUP, (g + 1) * POOL_GROUP)
        lo = c[:, P + 1 - w:P + 1 - w + T, sl]
        cnt = jnp.minimum(w, pos + 1).astype(jnp.float32)[None, :, None]
        outs.append((hi[..., sl] - lo) / cnt)
    pooled = jnp.concatenate(outs, axis=-1) - extf[:, P:]
    return pooled.astype(u.dtype), ext[:, -P:]


def pool_layer(x, prefix, pos, w_in, w_grp, scale, w_out, g_ln, b_ln):
    B, T, _ = x.shape
    z = x @ w_in
    u, gate = z[..., :D_POOL], z[..., D_POOL:]
    d, new_state = multiscale_pool(u, prefix, pos)
    d = jnp.einsum('btgc,gce->btge', d.reshape(B, T, N_POOL_GROUPS, POOL_GROUP), w_grp)
    d = d.reshape(B, T, D_POOL) * scale
    y = (d * jax.nn.silu(gate)) @ w_out
    return layer_norm(ALPHA * x + y, g_ln, b_ln), new_state


def shared_kv(h, pos, w_kv):
    B, T, _ = h.shape
    kv = h @ w_kv
    k = kv[..., :KV_DIM].reshape(B, T, N_KV_HEADS, HEAD_DIM)
    v = kv[..., KV_DIM:].reshape(B, T, N_KV_HEADS, HEAD_DIM)
    return rope_partial(k, pos), v


def attn_core(q, k, v, q_pos, k_pos, sinks):
    s = jnp.einsum('...qkgd,...skd->...kgqs', q, k).astype(jnp.float32) * (HEAD_DIM ** -0.5)
    qp = q_pos[..., :, None]
    kp = k_pos[..., None, :]
    mask = (kp <= qp) & (qp - kp < WINDOW) & (kp >= 0)
    s = jnp.where(mask[..., None, None, :, :], s, NEG)
    sink = sinks.astype(jnp.float32).reshape(N_KV_HEADS, GROUP)[:, :, None, None]
    m = jnp.maximum(jnp.max(s, axis=-1, keepdims=True), sink)
    p = jnp.exp(s - m)
    p = p / (jnp.sum(p, axis=-1, keepdims=True) + jnp.exp(sink - m))
    return jnp.einsum('...kgqs,...skd->...qkgd', p.astype(v.dtype), v)


def attn_layer(x, pos, kb, vb, k_pos, q_pos, blocked, w_in, sinks, w_out, g_ln, b_ln):
    B, T, _ = x.shape
    z = x @ w_in
    q = rope_partial(z[..., :D_ATTN].reshape(B, T, N_HEADS, HEAD_DIM), pos)
    gate = z[..., D_ATTN:]
    if blocked:
        nb = T // WINDOW
        q = q.reshape(B, nb, WINDOW, N_KV_HEADS, GROUP, HEAD_DIM)
    else:
        q = q.reshape(B, T, N_KV_HEADS, GROUP, HEAD_DIM)
    o = attn_core(q, kb, vb, q_pos, k_pos, sinks).reshape(B, T, D_ATTN)
    y = (o * jax.nn.silu(gate)) @ w_out
    return layer_norm(ALPHA * x + y, g_ln, b_ln)


def banded_blocks(a):
    B, S = a.shape[:2]
    nb = S // WINDOW
    ax = jnp.concatenate([jnp.zeros_like(a[:, :WINDOW]), a], axis=1)
    ax = ax.reshape(B, nb + 1, WINDOW, *a.shape[2:])
    return jnp.concatenate([ax[:, :-1], ax[:, 1:]], axis=2)


def setup_inputs(seed: int = 0) -> dict:
    key = jax.random.key(seed)
    ks = jax.random.split(key, 20)
    f32 = jnp.float32
    nrm = lambda k, shape, s: jax.random.normal(k, shape, f32) * s
    w_kv = jnp.concatenate([nrm(ks[9], (D_MODEL, KV_DIM), D_MODEL ** -0.5),
                            nrm(ks[10], (D_MODEL, KV_DIM), BETA * D_MODEL ** -0.5)], axis=1)
    return {
        "x_prompt": nrm(ks[0], (BATCH, SEQ, D_MODEL), 1.0),
        "x_sample": nrm(ks[1], (DEC_BATCH, DEC_SEQ, D_MODEL), 1.0),
        "state_pool": nrm(ks[2], (N_A_LAYERS, DEC_BATCH, POOL_STATE, D_POOL), 1.0),
        "cache_k": nrm(ks[3], (DEC_BATCH, WINDOW, N_KV_HEADS, HEAD_DIM), 1.0),
        "cache_v": nrm(ks[4], (DEC_BATCH, WINDOW, N_KV_HEADS, HEAD_DIM), BETA),
        "w_in_a": nrm(ks[5], (N_A_LAYERS, D_MODEL, 2 * D_POOL), D_MODEL ** -0.5),
        "w_grp_a": nrm(ks[6], (N_A_LAYERS, N_POOL_GROUPS, POOL_GROUP, POOL_GROUP), POOL_GROUP ** -0.5),
        "scale_a": 1.0 + nrm(ks[7], (N_A_LAYERS, D_POOL), 0.1),
        "w_out_a": nrm(ks[8], (N_A_LAYERS, D_POOL, D_MODEL), BETA * D_POOL ** -0.5),
        "w_kv": w_kv,
        "w_in_b": nrm(ks[11], (N_B_LAYERS, D_MODEL, 2 * D_ATTN), D_MODEL ** -0.5),
        "sinks_b": nrm(ks[12], (N_B_LAYERS, N_HEADS), 1.0),
        "w_out_b": nrm(ks[13], (N_B_LAYERS, D_ATTN, D_MODEL), BETA * D_ATTN ** -0.5),
        "ln_g": 1.0 + nrm(ks[14], (DEPTH, D_MODEL), 0.1),
        "ln_b": nrm(ks[15], (DEPTH, D_MODEL), 0.1),
    }


def reference(x_prompt, x_sample, state_pool, cache_k, cache_v, w_in_a, w_grp_a, scale_a, w_out_a,
              w_kv, w_in_b, sinks_b, w_out_b, ln_g, ln_b):
    S = x_prompt.shape[1]
    T = x_sample.shape[1]
    pos_p = jnp.arange(S, dtype=jnp.int32)
    pos_s = PAST_LEN + jnp.arange(T, dtype=jnp.int32)
    xp, xs = x_prompt, x_sample
    prefix_p = jnp.zeros((xp.shape[0], POOL_STATE, D_POOL), xp.dtype)
    pool_p, pool_s = [], []
    for i in range(DEPTH):
        if i < N_A_LAYERS:
            xp, sp = pool_layer(xp, prefix_p, pos_p, w_in_a[i], w_grp_a[i], scale_a[i], w_out_a[i], ln_g[i], ln_b[i])
            xs, ss = pool_layer(xs, state_pool[i], pos_s, w_in_a[i], w_grp_a[i], scale_a[i], w_out_a[i], ln_g[i], ln_b[i])
            pool_p.append(sp)
            pool_s.append(ss)
            if i == N_A_LAYERS - 1:
                k_p, v_p = shared_kv(xp, pos_p, w_kv)
                k_s, v_s = shared_kv(xs, pos_s, w_kv)
                kb_p, vb_p = banded_blocks(k_p), banded_blocks(v_p)
                kpos_x = jnp.arange(-WINDOW, S, dtype=jnp.int32).reshape(S // WINDOW + 1, WINDOW)
                kpos_p = jnp.concatenate([kpos_x[:-1], kpos_x[1:]], axis=1)
                qpos_p = pos_p.reshape(S // WINDOW, WINDOW)
                kx_s = jnp.concatenate([cache_k.astype(k_s.dtype), k_s], axis=1)
                vx_s = jnp.concatenate([cache_v.astype(v_s.dtype), v_s], axis=1)
                kpos_s = PAST_LEN - WINDOW + jnp.arange(WINDOW + T, dtype=jnp.int32)
        else:
            j = i - N_A_LAYERS
            xp = attn_layer(xp, pos_p, kb_p, vb_p, kpos_p, qpos_p, True,
                            w_in_b[j], sinks_b[j], w_out_b[j], ln_g[i], ln_b[i])
            xs = attn_layer(xs, pos_s, kx_s, vx_s, kpos_s, pos_s, False,
                            w_in_b[j], sinks_b[j], w_out_b[j], ln_g[i], ln_b[i])
    new_pool_prompt = jnp.stack(pool_p, axis=0)
    new_pool_sample = jnp.stack(pool_s, axis=0)
    new_k_prompt = k_p[:, -WINDOW:]
    new_v_prompt = v_p[:, -WINDOW:]
    new_k_sample = kx_s[:, -WINDOW:]
    new_v_sample = vx_s[:, -WINDOW:]
    return (xp, xs, new_pool_prompt, new_pool_sample, new_k_prompt, new_v_prompt, new_k_sample, new_v_sample)
```

```python
import numpy as np
import ml_dtypes
import concourse.bass as bass
import concourse.mybir as mybir
from concourse.bass_utils import run_bass_kernel_spmd
from contextlib import ExitStack

F32 = mybir.dt.float32
BF16 = mybir.dt.bfloat16
ALU = mybir.AluOpType
AF = mybir.ActivationFunctionType

D = 2048
NCH = 16
NT = 1188
S0, E0, H0, M0 = 0, 4, 36, 164
NMT = 8
BLK01 = [(0, 164), (164, 676), (676, 1188)]
BLK23 = [(0, 4), (164, 676), (676, 1188)]
ALPHA = 8.0 ** 0.25
LN_EPS = 1e-5
WINS = (2, 4, 8, 16)
PAST = 16384
DEBUG = False


def blk_of(c0):
    return 0 if c0 < 164 else (1 if c0 < 676 else 2)


class Tok:
    __slots__ = ("sem", "val", "eng", "dma")

    def __init__(self, sem, val, eng, dma):
        self.sem, self.val, self.eng, self.dma = sem, val, eng, dma


class Prog:
    KQ = 12

    def __init__(self, nc, stack):
        self.nc = nc
        self.h = {"pe": nc.tensor, "act": nc.scalar, "dve": nc.vector, "pool": nc.gpsimd, "sp": nc.sync}
        self.sem = {e: stack.enter_context(nc.semaphore("s_" + e)) for e in self.h}
        self.cnt = {e: 0 for e in self.h}
        self.waited = {e: {} for e in self.h}
        self.lastw = {}
        self.readers = {}
        self.dq = {}
        for q in ("sp", "pool", "act"):
            self.dq[q] = {"sems": [stack.enter_context(nc.semaphore("d_%s%d" % (q, i))) for i in range(self.KQ)], "n": 0}
        self.semid = {}
        self.all_toks = []
        self.out_toks = []

    def _wait(self, e, sem, val):
        key = id(sem)
        self.semid[key] = sem
        if self.waited[e].get(key, 0) < val:
            self.h[e].wait_ge(sem, val)
            self.waited[e][key] = val

    def op(self, e, fn, reads=(), writes=(), dma=False, extra=()):
        deps = list(extra)
        psr = [k for k in reads if isinstance(k, tuple) and k[0] == "PS"]
        if psr:
            reads = [k for k in reads if k not in psr]
            writes = list(writes) + psr
        for k in reads:
            t = self.lastw.get(k)
            if t is not None:
                deps.append(t)
        for k in writes:
            t = self.lastw.get(k)
            if t is not None:
                deps.append(t)
            deps.extend(self.readers.get(k, ()))
        need = {}
        for t in deps:
            if e == "pe" and (not dma) and t.eng == "pe" and (not t.dma):
                continue
            key = id(t.sem)
            self.semid[key] = t.sem
            if need.get(key, 0) < t.val:
                need[key] = t.val
        for key, val in need.items():
            self._wait(e, self.semid[key], val)
        if dma:
            q = self.dq[e]
            j = q["n"]
            q["n"] += 1
            sem = q["sems"][j % self.KQ]
            if j >= self.KQ:
                self._wait(e, sem, 16 * (j // self.KQ))
            ins = fn(self.h[e])
            val = 16 * (j // self.KQ + 1)
            ins.then_inc(sem, 16)
            tok = Tok(sem, val, e, True)
        else:
            ins = fn(self.h[e])
            self.cnt[e] += 1
            ins.then_inc(self.sem[e], 1)
            tok = Tok(self.sem[e], self.cnt[e], e, False)
        for k in reads:
            self.readers.setdefault(k, []).append(tok)
        for k in writes:
            self.lastw[k] = tok
            self.readers[k] = []
        self.all_toks.append(tok)
        if len(self.all_toks) > 4096:
            self._compact()
        return tok

    def _compact(self):
        best = {}
        for t in self.all_toks:
            k = id(t.sem)
            if k not in best or best[k].val < t.val:
                best[k] = t
        self.all_toks = list(best.values())

    def barrier(self):
        self._compact()
        toks = list(self.all_toks)
        for e in self.h:
            for t in toks:
                self._wait(e, t.sem, t.val)
        self.lastw = {}
        self.readers = {}

    def final_wait(self, e="sp"):
        self._compact()
        for t in self.all_toks:
            self._wait(e, t.sem, t.val)


def bcast(ap, shape, axis):
    return ap.unsqueeze(axis).broadcast_to(shape)


class _Stop(Exception):
    pass


def build(debug=False, stage=None):
    nc = bass.Bass("TRN2", target_bir_lowering=False, dynamic_dma_scratch_size=4096)

    def din(name, shape, dt=F32):
        return nc.dram_tensor(name, list(shape), dt, kind="ExternalInput").ap()

    def dout(name, shape, dt=F32):
        return nc.dram_tensor(name, list(shape), dt, kind="ExternalOutput").ap()

    xin = din("xin", [NT, D])
    stp = din("stp", [2, 60, D])
    ck = din("ck", [4, 128, 256])
    cv = din("cv", [4, 128, 256])
    w_in_a = din("w_in_a", [2, D, 2 * D])
    w_grp_a = din("w_grp_a", [2, 4, 512, 512])
    scale_a = din("scale_a", [2, D])
    w_out_a = din("w_out_a", [2, D, D])
    w_kv = din("w_kv", [D, 512])
    w_in_b = din("w_in_b", [2, D, 2 * D])
    sinks_b = din("sinks_b", [2, 32])
    w_out_b = din("w_out_b", [2, D, D])
    ln_g = din("ln_g", [4, D])
    ln_b = din("ln_b", [4, D])
    c_cs = din("c_cs", [128, 10, 16])
    c_mask = din("c_mask", [128, 3, 512], BF16)
    c_hm = din("c_hm", [128, 1])
    c_pw = din("c_pw", [60, 4, 4])
    c_rfix = din("c_rfix", [128, 4, 15])

    o_yp = dout("o_yp", [1024, D])
    o_ys = dout("o_ys", [4, D])
    o_npp = dout("o_npp", [2, 15, D])
    o_nps = dout("o_nps", [2, 4, 15, D])
    o_nkp = dout("o_nkp", [128, 256])
    o_nvp = dout("o_nvp", [128, 256])
    o_nks = dout("o_nks", [4, 128, 256])
    o_nvs = dout("o_nvs", [4, 128, 256])
    o_dbg = dout("o_dbg", [4, 128, NCH, NT]) if debug else None

    with ExitStack() as st:
        P = Prog(nc, st)

        def sb(name, shape, dt=F32, stack=st):
            return stack.enter_context(nc.sbuf_tensor(name, list(shape), dt))

        X32T = sb("X32T", [128, NCH, NT])
        XT = sb("XT", [128, NCH, NT], BF16)
        HT = sb("HT", [128, NCH, NT], BF16)
        WB = [sb("WB%d" % i, [128, 4096], BF16) for i in range(3)]
        IDB = sb("IDB", [128, 128], BF16)
        IDF = sb("IDF", [128, 128])
        ONESM = sb("ONESM", [128, 128], BF16)
        ONES64 = sb("ONES64", [128, 64], BF16)
        EPS = sb("EPS", [128, 1])
        HM = sb("HM", [128, 1])
        GAM = sb("GAM", [128, 4, NCH])
        BET = sb("BET", [128, 4, NCH])
        SCA = sb("SCA", [128, 2, NCH])
        CS = sb("CS", [128, 10, 16])
        MASK = sb("MASK", [128, 3, 512], BF16)
        PW = sb("PW", [60, 4, 4])
        RFIX = sb("RFIX", [128, 4, 15])
        RR = [sb("RR%d" % i, [128, 512]) for i in range(2)]
        PS = [st.enter_context(nc.psum_tensor("ps%d" % i, [128, 512], F32)) for i in range(8)]
        bank_ctr = [0]

        def next_bank():
            b = bank_ctr[0] % 8
            bank_ctr[0] += 1
            return b

        P.op("sp", lambda e: e.dma_start(out=HM[:], in_=c_hm[:, :]), writes=["HM"], dma=True)
        P.op("sp", lambda e: e.dma_start(out=CS[:], in_=c_cs[:, :, :]), writes=["CS"], dma=True)
        P.op("sp", lambda e: e.dma_start(out=MASK[:], in_=c_mask[:, :, :]), writes=["MASK"], dma=True)
        P.op("sp", lambda e: e.dma_start(out=PW[:], in_=c_pw[:, :, :]), writes=["PW"], dma=True)
        P.op("sp", lambda e: e.dma_start(out=RFIX[:], in_=c_rfix[:, :, :]), writes=["RFIX"], dma=True)
        P.op("pool", lambda e: e.memset(IDF[:], 0.0), writes=["IDF"])
        P.op("pool", lambda e: e.iota(IDF[:], [[1, 128]], base=0, channel_multiplier=-1, allow_small_or_imprecise_dtypes=True), writes=["IDF"])
        P.op("dve", lambda e: e.tensor_scalar(out=IDF[:], in0=IDF[:], scalar1=0.0, scalar2=None, op0=ALU.is_equal), writes=["IDF"])
        P.op("dve", lambda e: e.tensor_copy(out=IDB[:], in_=IDF[:]), reads=["IDF"], writes=["IDB"])
        P.op("dve", lambda e: e.memset(ONESM[:], 1.0 / D), writes=["ONESM"])
        with ExitStack() as sprm:
            PRM = sb("PRM", [64, 3, 128], F32, sprm)
            P.op("sp", lambda e: e.dma_start(out=PRM[0:64, 0, :], in_=ln_g.rearrange("l (c p) -> (l c) p", p=128)), writes=["PRM0"], dma=True)
            P.op("sp", lambda e: e.dma_start(out=PRM[0:64, 1, :], in_=ln_b.rearrange("l (c p) -> (l c) p", p=128)), writes=["PRM1"], dma=True)
            P.op("sp", lambda e: e.dma_start(out=PRM[0:32, 2, :], in_=scale_a.rearrange("l (c p) -> (l c) p", p=128)), writes=["PRM2"], dma=True)

            def trp(e):
                e.transpose(out=PS[0][:, 0:64], in_=PRM[0:64, 0, :], identity=IDF[0:64, 0:64])
                e.transpose(out=PS[0][:, 64:128], in_=PRM[0:64, 1, :], identity=IDF[0:64, 0:64])
                return e.transpose(out=PS[0][:, 128:160], in_=PRM[0:32, 2, :], identity=IDF[0:32, 0:32])
            P.op("pe", trp, reads=["PRM0", "PRM1", "PRM2", "IDF"], writes=[("PS", 0)])
            P.op("dve", lambda e: e.tensor_copy(out=GAM[:].rearrange("p l c -> p (l c)"), in_=PS[0][:, 0:64]), reads=[("PS", 0)], writes=["GAM"])
            P.op("dve", lambda e: e.tensor_copy(out=BET[:].rearrange("p l c -> p (l c)"), in_=PS[0][:, 64:128]), reads=[("PS", 0)], writes=["BET"])
            P.op("dve", lambda e: e.tensor_copy(out=SCA[:].rearrange("p l c -> p (l c)"), in_=PS[0][:, 128:160]), reads=[("PS", 0)], writes=["SCA"])
            P.barrier()
        P.op("dve", lambda e: e.memset(ONES64[:], 1.0), writes=["ONES64"])
        P.op("dve", lambda e: e.memset(EPS[:], LN_EPS), writes=["EPS"])

        def wview(buf, nchunk, ncol):
            return WB[buf][:, 0:nchunk * ncol].rearrange("p (c f) -> p c f", c=nchunk)

        wstate = {"i": 0, "pending": None}
        wlist = []

        def wissue(idx):
            ap2d, nchunk, ncol = wlist[idx]
            buf = idx % 3
            dst = wview(buf, nchunk, ncol)
            src = ap2d.rearrange("(c p) f -> p c f", p=128)
            P.op("pool", lambda e: e.dma_start(out=dst, in_=src), writes=[("W", buf)], dma=True)

        wstate["issued"] = -1

        def wensure(upto):
            while wstate["issued"] < min(upto, len(wlist) - 1):
                wstate["issued"] += 1
                wissue(wstate["issued"])

        def wnext(prefetch=True):
            i = wstate["i"]
            wensure(i + 1 if prefetch else i)
            wstate["i"] = i + 1
            ap2d, nchunk, ncol = wlist[i]
            return wview(i % 3, nchunk, ncol), ("W", i % 3)

        def wprefetch():
            wensure(wstate["issued"] + 1)

        for l in range(2):
            for j in range(8):
                wlist.append((w_in_a[l, :, D + 256 * j:D + 256 * (j + 1)], 16, 256))
            for g in range(4):
                for i in range(2):
                    wlist.append((w_in_a[l, :, 512 * g + 256 * i:512 * g + 256 * (i + 1)], 16, 256))
                wlist.append((w_grp_a[l, g, :, :], 4, 512))
            for j in range(8):
                wlist.append((w_out_a[l, :, 256 * j:256 * (j + 1)], 16, 256))
        wlist.append((w_kv[:, 256:512], 16, 256))
        wlist.append((w_kv[:, 0:256], 16, 256))
        for l in range(2):
            for j in range(8):
                wlist.append((w_in_b[l, :, D + 256 * j:D + 256 * (j + 1)], 16, 256))
            for j in range(8):
                wlist.append((w_in_b[l, :, 256 * j:256 * (j + 1)], 16, 256))
            for j in range(8):
                wlist.append((w_out_b[l, :, 256 * j:256 * (j + 1)], 16, 256))

        allch = list(range(NCH))

        def keys(name, chs, b):
            return [(name, c, b) for c in chs]

        def modeF_chunk(wv, wkey, fsub, nK, rhs_of, rhs_keys_of, blocks, evac):
            for (c0, c1) in blocks:
                n = c1 - c0
                b = blk_of(c0)
                bank = next_bank()

                def mm(e, c0=c0, c1=c1, n=n, bank=bank):
                    ins = None
                    for k in range(nK):
                        ins = e.matmul(PS[bank][:, 0:n], lhsT=wv[:, k, fsub * 128:(fsub + 1) * 128], rhs=rhs_of(k, c0, c1),
                                       start=(k == 0), stop=(k == nK - 1))
                    return ins
                P.op("pe", mm, reads=[wkey] + rhs_keys_of(b), writes=[("PS", bank)])
                evac(bank, c0, c1, b)

        def layer_norm(l, blocks):
            nb = len(blocks)
            info = [(c0, c1, c1 - c0, blk_of(c0)) for (c0, c1) in blocks]
            bms = []
            for (c0, c1, n, b) in info:
                bm = next_bank()
                bms.append(bm)

                def mm_mu(e, bm=bm, c0=c0, c1=c1, n=n):
                    ins = None
                    for k in range(NCH):
                        ins = e.matmul(PS[bm][:, 0:n], lhsT=ONESM[:], rhs=XT[:, k, c0:c1], start=(k == 0), stop=(k == NCH - 1))
                    return ins
                P.op("pe", mm_mu, reads=keys("XT", allch, b) + ["ONESM"], writes=[("PS", bm)])
            for (c0, c1, n, b), bm in zip(info, bms):
                P.op("dve", lambda e: e.tensor_tensor(out=X32T[:, :, c0:c1], in0=X32T[:, :, c0:c1],
                                                      in1=bcast(PS[bm][:, 0:n], [128, NCH, n], 1), op=ALU.subtract),
                     reads=[("PS", bm)], writes=keys("X32", allch, b))
            CS_ = 12
            for (c0, c1, n, b) in info:
                P.op("act", lambda e: e.activation(out=XT[:, 0:CS_, c0:c1], in_=X32T[:, 0:CS_, c0:c1], func=AF.Square),
                     reads=keys("X32", range(CS_), b), writes=keys("XT", range(CS_), b))
                P.op("dve", lambda e: e.tensor_tensor(out=XT[:, CS_:NCH, c0:c1], in0=X32T[:, CS_:NCH, c0:c1], in1=X32T[:, CS_:NCH, c0:c1], op=ALU.mult),
                     reads=keys("X32", range(CS_, NCH), b), writes=keys("XT", range(CS_, NCH), b))
            bvs = []
            for (c0, c1, n, b) in info:
                bv = next_bank()
                bvs.append(bv)

                def mm_var(e, bv=bv, c0=c0, c1=c1, n=n):
                    ins = None
                    for k in range(NCH):
                        ins = e.matmul(PS[bv][:, 0:n], lhsT=ONESM[:], rhs=XT[:, k, c0:c1], start=(k == 0), stop=(k == NCH - 1))
                    return ins
                P.op("pe", mm_var, reads=keys("XT", allch, b) + ["ONESM"], writes=[("PS", bv)])
            for bi, ((c0, c1, n, b), bv) in enumerate(zip(info, bvs)):
                R = RR[bi % 2]
                rk = ("RR", bi % 2)
                P.op("act", lambda e: e.activation(out=R[:, 0:n], in_=PS[bv][:, 0:n], func=AF.Sqrt, bias=EPS[:], scale=1.0),
                     reads=[("PS", bv), "EPS"], writes=[rk])
                P.op("dve", lambda e: e.reciprocal(out=R[:, 0:n], in_=R[:, 0:n]), writes=[rk])
                P.op("dve", lambda e: e.tensor_tensor(out=X32T[:, :, c0:c1], in0=X32T[:, :, c0:c1],
                                                      in1=bcast(R[:, 0:n], [128, NCH, n], 1), op=ALU.mult),
                     reads=[rk], writes=keys("X32", allch, b))
                for c in range(NCH):
                    if c < 8:
                        P.op("dve", lambda e, c=c: e.tensor_scalar(out=X32T[:, c, c0:c1], in0=X32T[:, c, c0:c1],
                                                                   scalar1=GAM[:, l, c:c + 1], scalar2=BET[:, l, c:c + 1],
                                                                   op0=ALU.mult, op1=ALU.add),
                             reads=["GAM", "BET"], writes=[("X32", c, b)])
                    else:
                        P.op("act", lambda e, c=c: e.activation(out=X32T[:, c, c0:c1], in_=X32T[:, c, c0:c1], func=AF.Identity,
                                                                scale=GAM[:, l, c:c + 1], bias=BET[:, l, c:c + 1]),
                             reads=["GAM", "BET"], writes=[("X32", c, b)])
                P.op("dve", lambda e: e.tensor_copy(out=XT[:, 0:5, c0:c1], in_=X32T[:, 0:5, c0:c1]),
                     reads=keys("X32", range(5), b), writes=keys("XT", range(5), b))
                P.op("act", lambda e: e.copy(out=XT[:, 5:NCH, c0:c1], in_=X32T[:, 5:NCH, c0:c1]),
                     reads=keys("X32", range(5, NCH), b), writes=keys("XT", range(5, NCH), b))

        def wout_phase(l, blocks):
            for j in range(8):
                wv, wkey = wnext()
                for fsub in range(2):
                    c = 2 * j + fsub

                    def evac(bank, c0, c1, b, c=c):
                        n = c1 - c0
                        P.op("dve", lambda e: e.scalar_tensor_tensor(out=X32T[:, c, c0:c1], in0=X32T[:, c, c0:c1], scalar=ALPHA,
                                                                     in1=PS[bank][:, 0:n], op0=ALU.mult, op1=ALU.add),
                             reads=[("PS", bank)], writes=[("X32", c, b)])
                        P.op("act", lambda e: e.copy(out=XT[:, c, c0:c1], in_=X32T[:, c, c0:c1]), reads=[("X32", c, b)], writes=[("XT", c, b)])
                    modeF_chunk(wv, wkey, fsub, NCH, lambda k, c0, c1: HT[:, k, c0:c1], lambda b: keys("HT", allch, b), blocks, evac)
            layer_norm(l, blocks)

        def gate_phase(blocks, stagger):
            def piece(wv, wkey, j, bl):
                for fsub in range(2):
                    c = 2 * j + fsub

                    def evac(bank, c0, c1, b, c=c):
                        n = c1 - c0
                        P.op("act", lambda e: e.activation(out=HT[:, c, c0:c1], in_=PS[bank][:, 0:n], func=AF.Silu),
                             reads=[("PS", bank)], writes=[("HT", c, b)])
                    modeF_chunk(wv, wkey, fsub, NCH, lambda k, c0, c1: XT[:, k, c0:c1], lambda b: keys("XT", allch, b), bl, evac)
            j0 = 0
            if stagger:
                ws = [wnext(), wnext(), wnext(prefetch=False)]
                for j in range(3):
                    piece(ws[j][0], ws[j][1], j, blocks[:-1])
                for j in range(3):
                    piece(ws[j][0], ws[j][1], j, blocks[-1:])
                    if j == 0:
                        wprefetch()
                j0 = 3
            for j in range(j0, 8):
                wv, wkey = wnext()
                piece(wv, wkey, j, blocks)

        def dump(idx):
            if debug:
                P.barrier()
                P.op("sp", lambda e: e.dma_start(out=o_dbg[idx, :, :, :], in_=X32T[:]), dma=True)
                P.barrier()

        def ckpt(name):
            return stage == name

        def phases():
            with ExitStack() as s0:
                XIN = [sb("XIN%d" % i, [128, D], F32, s0) for i in range(2)]
                rt = 0
                r0 = 0
                while r0 < NT:
                    r1 = min(r0 + 128, NT)
                    n = r1 - r0
                    xi = XIN[rt % 2]
                    xk = ("XIN", rt % 2)
                    P.op("sp", lambda e, xi=xi, r0=r0, r1=r1, n=n: e.dma_start(out=xi[0:n, :], in_=xin[r0:r1, :]), writes=[xk], dma=True)
                    for cg in range(4):
                        bank = next_bank()

                        def tr(e, xi=xi, n=n, cg=cg, bank=bank):
                            ins = None
                            for cc in range(4):
                                c = cg * 4 + cc
                                ins = e.transpose(out=PS[bank][:, cc * 128:cc * 128 + n], in_=xi[0:n, c * 128:(c + 1) * 128], identity=IDF[0:n, 0:n])
                            return ins
                        P.op("pe", tr, reads=[xk, "IDF"], writes=[("PS", bank)])
                        src = PS[bank][:].rearrange("p (c t) -> p c t", c=4)[:, :, 0:n]
                        P.op("dve", lambda e, src=src, cg=cg, r0=r0, r1=r1: e.tensor_copy(out=X32T[:, cg * 4:cg * 4 + 4, r0:r1], in_=src),
                             reads=[("PS", bank)], writes=[("X32p0", cg, rt)])
                        P.op("act", lambda e, cg=cg, r0=r0, r1=r1: e.copy(out=XT[:, cg * 4:cg * 4 + 4, r0:r1], in_=X32T[:, cg * 4:cg * 4 + 4, r0:r1]),
                             reads=[("X32p0", cg, rt)], writes=[("XTp0", cg, rt)])
                    rt += 1
                    r0 = r1
                P.barrier()
                if ckpt('p0'):
                    return

            o_toks = []
            with ExitStack() as sa:
                US = [sb("U%d" % i, [128, NT], F32, sa) for i in range(2)]
                TA = sb("TA", [128, NT], F32, sa)
                TB = sb("TB", [128, NT], F32, sa)
                DT = sb("DT", [128, 4, NT], BF16, sa)
                STT = sb("STT", [60, 512], F32, sa)
                UKS = sb("UKS", [128, NCH, 20], F32, sa)
                FX = sb("FX", [128, 16], F32, sa)
                PST = sb("PST", [20, 1024], F32, sa)
                P.op("pool", lambda e: e.memset(DT[:], 0.0), writes=keys("DT", range(4), 0) + keys("DT", range(4), 1) + keys("DT", range(4), 2))
                for l in range(2):
                    blocks = BLK01
                    gate_phase(blocks, l >= 1)
                    if ckpt('gate%d' % l):
                        return
                    for g in range(4):
                        w = WINS[g]
                        P.op("sp", lambda e, g=g: e.dma_start(out=STT[:, :], in_=stp[l, :, 512 * g:512 * (g + 1)]), writes=["STT"], dma=True)
                        for i in range(2):
                            wv, wkey = wnext()
                            for fsub in range(2):
                                cc = 2 * i + fsub
                                c = 4 * g + cc
                                U = US[c % 2]
                                up = c % 2

                                def evac(bank, c0, c1, b):
                                    n = c1 - c0
                                    if b == 0:
                                        P.op("act", lambda e: e.copy(out=U[:, 0:4], in_=PS[bank][:, 0:4]), reads=[("PS", bank)], writes=[("U", up, 0)])
                                        P.op("act", lambda e: e.activation(out=U[:, 4:164], in_=PS[bank][:, 4:164], func=AF.Identity, scale=HM[:]),
                                             reads=[("PS", bank), "HM"], writes=[("U", up, 0)])
                                    else:
                                        P.op("act", lambda e: e.copy(out=U[:, c0:c1], in_=PS[bank][:, 0:n]), reads=[("PS", bank)], writes=[("U", up, b)])
                                modeF_chunk(wv, wkey, fsub, NCH, lambda k, c0, c1: XT[:, k, c0:c1], lambda b: keys("XT", allch, b), blocks, evac)
                                ukeys = [("U", up, 0), ("U", up, 1), ("U", up, 2)]
                                P.op("pool", lambda e, c=c: e.tensor_copy(out=UKS[:, c, 0:4], in_=U[:, 0:4]), reads=ukeys, writes=[("UKS", c)])
                                P.op("pool", lambda e, c=c: e.tensor_copy(out=UKS[:, c, 4:20], in_=U[:, NT - 16:NT]), reads=ukeys, writes=[("UKS", c)])
                                bp = next_bank()
                                P.op("pe", lambda e, bp=bp, cc=cc, g=g: e.matmul(PS[bp][:, 0:4], lhsT=STT[0:60, cc * 128:(cc + 1) * 128], rhs=PW[0:60, g, :],
                                                                               start=True, stop=True),
                                     reads=["STT", "PW"], writes=[("PS", bp)])
                                P.op("dve", lambda e: e.tensor_tensor(out=TA[:, 5:NT], in0=U[:, 5:NT], in1=U[:, 4:NT - 1], op=ALU.add), reads=ukeys, writes=["TA"])
                                Sb, Sk = TA, "TA"
                                if w >= 4:
                                    P.op("dve", lambda e: e.tensor_tensor(out=TB[:, 7:NT], in0=TA[:, 7:NT], in1=TA[:, 5:NT - 2], op=ALU.add), reads=["TA"], writes=["TB"])
                                    Sb, Sk = TB, "TB"
                                if w >= 8:
                                    P.op("dve", lambda e: e.tensor_tensor(out=TA[:, 11:NT], in0=TB[:, 11:NT], in1=TB[:, 7:NT - 4], op=ALU.add), reads=["TB"], writes=["TA"])
                                    Sb, Sk = TA, "TA"
                                if w >= 16:
                                    P.op("dve", lambda e: e.tensor_tensor(out=TB[:, 19:NT], in0=TA[:, 19:NT], in1=TA[:, 11:NT - 8], op=ALU.add), reads=["TA"], writes=["TB"])
                                    Sb, Sk = TB, "TB"
                                dk = keys("DT", [cc], 0) + keys("DT", [cc], 1) + keys("DT", [cc], 2)
                                P.op("dve", lambda e, Sb=Sb, cc=cc, w=w: e.scalar_tensor_tensor(out=DT[:, cc, 19:NT], in0=Sb[:, 19:NT], scalar=1.0 / w,
                                                                                         in1=U[:, 19:NT], op0=ALU.mult, op1=ALU.subtract),
                                     reads=[Sk] + ukeys, writes=dk)
                                P.op("dve", lambda e, Sb=Sb, g=g: e.tensor_tensor(out=FX[:, 0:15], in0=Sb[:, M0:M0 + 15], in1=RFIX[:, g, :], op=ALU.mult),
                                     reads=[Sk, "RFIX"], writes=["FX"])
                                P.op("dve", lambda e, cc=cc: e.tensor_tensor(out=DT[:, cc, M0:M0 + 15], in0=FX[:, 0:15], in1=U[:, M0:M0 + 15], op=ALU.subtract),
                                     reads=["FX"] + ukeys, writes=dk)
                                P.op("dve", lambda e, cc=cc, w=w, bp=bp: e.scalar_tensor_tensor(out=DT[:, cc, 0:4], in0=U[:, 0:4], scalar=(1.0 / w - 1.0),
                                                                                         in1=PS[bp][:, 0:4], op0=ALU.mult, op1=ALU.add),
                                     reads=[("PS", bp)] + ukeys, writes=dk)
                        wv, wkey = wnext()
                        for ee in range(4):
                            c = 4 * g + ee

                            def evac(bank, c0, c1, b, c=c):
                                n = c1 - c0
                                P.op("dve", lambda e: e.scalar_tensor_tensor(out=HT[:, c, c0:c1], in0=PS[bank][:, 0:n], scalar=SCA[:, l, c:c + 1],
                                                                             in1=HT[:, c, c0:c1], op0=ALU.mult, op1=ALU.mult),
                                     reads=[("PS", bank), "SCA"], writes=[("HT", c, b)])
                            modeF_chunk(wv, wkey, ee, 4, lambda k, c0, c1: DT[:, k, c0:c1], lambda b: keys("DT", range(4), b), blocks, evac)
                    if ckpt('grp%d' % l):
                        return
                    for hf in range(2):
                        for q2 in range(2):
                            q4 = hf * 2 + q2
                            bank = next_bank()

                            def tr(e, q4=q4, bank=bank):
                                ins = None
                                for cc in range(4):
                                    c = q4 * 4 + cc
                                    ins = e.transpose(out=PS[bank][0:20, cc * 128:(cc + 1) * 128], in_=UKS[:, c, :], identity=IDF[:])
                                return ins
                            P.op("pe", tr, reads=[("UKS", q4 * 4 + i) for i in range(4)] + ["IDF"], writes=[("PS", bank)])
                            P.op("act", lambda e, q2=q2, bank=bank: e.copy(out=PST[0:20, q2 * 512:(q2 + 1) * 512], in_=PS[bank][0:20, :]),
                                 reads=[("PS", bank)], writes=[("PST", q2)])
                        pk = [("PST", 0), ("PST", 1)]
                        o_toks.append(P.op("sp", lambda e, hf=hf: e.dma_start(out=o_npp[l, :, hf * 1024:(hf + 1) * 1024], in_=PST[5:20, :]), reads=pk, dma=True))
                        for s in range(4):
                            o_toks.append(P.op("sp", lambda e, s=s, hf=hf: e.dma_start(out=o_nps[l, s, 14:15, hf * 1024:(hf + 1) * 1024], in_=PST[s:s + 1, :]), reads=pk, dma=True))
                    for s in range(4):
                        o_toks.append(P.op("sp", lambda e, s=s: e.dma_start(out=o_nps[l, s, 0:14, :], in_=stp[l, 15 * s + 1:15 * s + 15, :]), dma=True))
                    if ckpt('np%d' % l):
                        return
                    wout_phase(l, blocks)
                    dump(l)
                    if ckpt('L%d' % l):
                        return
                P.barrier()

            with ExitStack() as sbk:
                KTP = [sb("KTP%d" % i, [128, 4, 9 * 128], BF16, sbk) for i in range(2)]
                P.op("pool", lambda e: e.memset(KTP[0][:], 0.0), writes=[("KT", i) for i in range(9)])
                P.op("pool", lambda e: e.memset(KTP[1][:], 0.0), writes=[("KT", i) for i in range(9)])
                VB = sb("VB", [128, 9, 256], BF16, sbk)
                KTS = sb("KTS", [128, 4, 4, 128], BF16, sbk)
                VSB = sb("VSB", [128, 4, 256], BF16, sbk)
                RTS = [sb("RT%d" % i, [128, 4, 4, 8], F32, sbk) for i in range(2)]
                ktiles = [(H0, H0 + 128)] + [(M0 + 128 * i, M0 + 128 * (i + 1)) for i in range(NMT)]
                tiles_kv = [(S0, S0 + 4, 0, None)] + [(a, b_, i + 1, i) for i, (a, b_) in enumerate(ktiles)]

                def rope(src3, dst3, n, csidx, nh, sk, dk, rs=0):
                    RT = RTS[rs]
                    r0k, r1k, r2k, r3k = ("RT0", rs), ("RT1", rs), ("RT2", rs), ("RT3", rs)
                    cosb = bcast(CS[0:n, csidx, 0:8], [n, nh, 8], 1)
                    sinb = bcast(CS[0:n, csidx, 8:16], [n, nh, 8], 1)
                    rd = ["RT"]
                    P.op("act", lambda e: e.copy(out=dst3[:, :, 16:64], in_=src3[:, :, 16:64]), reads=[sk], writes=[dk])
                    P.op("dve", lambda e: e.tensor_tensor(out=RT[0:n, 0, 0:nh, :], in0=src3[:, :, 0:8], in1=cosb, op=ALU.mult), reads=[sk, "CS"], writes=[r0k])
                    P.op("dve", lambda e: e.tensor_tensor(out=RT[0:n, 1, 0:nh, :], in0=src3[:, :, 8:16], in1=sinb, op=ALU.mult), reads=[sk, "CS"], writes=[r1k])
                    P.op("dve", lambda e: e.tensor_tensor(out=RT[0:n, 2, 0:nh, :], in0=src3[:, :, 8:16], in1=cosb, op=ALU.mult), reads=[sk, "CS"], writes=[r2k])
                    P.op("dve", lambda e: e.tensor_tensor(out=RT[0:n, 3, 0:nh, :], in0=src3[:, :, 0:8], in1=sinb, op=ALU.mult), reads=[sk, "CS"], writes=[r3k])
                    P.op("dve", lambda e: e.tensor_tensor(out=dst3[:, :, 0:8], in0=RT[0:n, 0, 0:nh, :], in1=RT[0:n, 1, 0:nh, :], op=ALU.subtract),
                         reads=[r0k, r1k], writes=[dk])
                    P.op("dve", lambda e: e.tensor_tensor(out=dst3[:, :, 8:16], in0=RT[0:n, 2, 0:nh, :], in1=RT[0:n, 3, 0:nh, :], op=ALU.add),
                         reads=[r2k, r3k], writes=[dk])

                with ExitStack() as skv:
                    HTF = HT[:, 0:4, :].rearrange("p c t -> p (c t)").bitcast(F32)
                    KS = HTF[:, 0:1024].rearrange("p (s f) -> p s f", s=4)
                    VS = HTF[:, 1024:2048].rearrange("p (s f) -> p s f", s=4)
                    KRS = [sb("KR%d" % i, [128, 256], F32, skv) for i in range(2)]
                    KBDS = [sb("KBD%d" % i, [128, 4, 128], BF16, skv) for i in range(2)]
                    VR = sb("VR", [128, 256], F32, skv)
                    for s in range(4):
                        P.op("sp", lambda e, s=s: e.dma_start(out=KS[0:127, s, :], in_=ck[s, 1:128, :]), writes=[("KS", s)], dma=True)
                        P.op("sp", lambda e, s=s: e.dma_start(out=VS[0:127, s, :], in_=cv[s, 1:128, :]), writes=[("VS", s)], dma=True)
                    wv, wkey = wnext()
                    for (a, b_, csidx, kt) in tiles_kv:
                        n = b_ - a
                        bk = blk_of(a)
                        bank = next_bank()

                        def mmv(e, a=a, b_=b_, n=n, bank=bank):
                            ins = None
                            for k in range(NCH):
                                ins = e.matmul(PS[bank][0:n, 0:256], lhsT=XT[:, k, a:b_], rhs=wv[:, k, :], start=(k == 0), stop=(k == NCH - 1))
                            return ins
                        P.op("pe", mmv, reads=[wkey] + keys("XT", allch, bk), writes=[("PS", bank)])
                        if kt is None:
                            P.op("act", lambda e, bank=bank: e.copy(out=VR[0:4, :], in_=PS[bank][0:4, 0:256]), reads=[("PS", bank)], writes=["VR"])
                            for s in range(4):
                                P.op("sp", lambda e, s=s: e.dma_start(out=VS[127:128, s, :], in_=VR[s:s + 1, :]), reads=["VR"], writes=[("VS", s)], dma=True)
                        else:
                            P.op("act", lambda e, bank=bank, kt=kt: e.copy(out=VB[:, kt, :], in_=PS[bank][:, 0:256]), reads=[("PS", bank)], writes=[("VB", kt)])
                            if kt == 8:
                                P.op("dve", lambda e, bank=bank: e.tensor_copy(out=VR[:, :], in_=PS[bank][:, 0:256]), reads=[("PS", bank)], writes=["VR"])
                                o_toks.append(P.op("sp", lambda e: e.dma_start(out=o_nvp[:, :], in_=VR[:, :]), reads=["VR"], dma=True))
                    for s in range(4):
                        o_toks.append(P.op("sp", lambda e, s=s: e.dma_start(out=o_nvs[s, :, :], in_=VS[:, s, :]), reads=[("VS", s)], dma=True))
                        P.op("pool", lambda e, s=s: e.tensor_copy(out=VSB[:, s, :], in_=VS[:, s, :]), reads=[("VS", s)], writes=[("VSB", s)])
                    wv, wkey = wnext()

                    def kmm(tix):
                        (a, b_, csidx, kt) = tiles_kv[tix]
                        KR = KRS[tix % 2]
                        krk = ("KR", tix % 2)
                        n = b_ - a
                        bk = blk_of(a)
                        bank = next_bank()

                        def mmk(e):
                            ins = None
                            for k in range(NCH):
                                ins = e.matmul(PS[bank][0:n, 0:256], lhsT=XT[:, k, a:b_], rhs=wv[:, k, :], start=(k == 0), stop=(k == NCH - 1))
                            return ins
                        P.op("pe", mmk, reads=[wkey] + keys("XT", allch, bk), writes=[("PS", bank)])
                        src3 = PS[bank][0:n, 0:256].rearrange("p (h d) -> p h d", h=4)
                        dst3 = KR[0:n, :].rearrange("p (h d) -> p h d", h=4)
                        rope(src3, dst3, n, csidx, 4, ("PS", bank), krk, tix % 2)
                        if kt is None:
                            for s in range(4):
                                P.op("sp", lambda e, s=s: e.dma_start(out=KS[127:128, s, :], in_=KR[s:s + 1, :]), reads=[krk], writes=[("KS", s)], dma=True)
                        else:
                            if kt == 8:
                                o_toks.append(P.op("sp", lambda e: e.dma_start(out=o_nkp[:, :], in_=KR[:, :]), reads=[krk], dma=True))
                            KBD = KBDS[tix % 2]
                            KBD4 = KBD[:].rearrange("p g (a d) -> p g a d", a=2)
                            P.op("pool", lambda e: e.tensor_copy(out=KBD4, in_=bcast(dst3, [128, 4, 2, 64], 2)), reads=[krk], writes=[("KBD", tix % 2)])

                    def ktr(tix):
                        (a, b_, csidx, kt) = tiles_kv[tix]
                        if kt is None:
                            return
                        KBD = KBDS[tix % 2]
                        bt = next_bank()
                        pbt = PS[bt][:].bitcast(BF16)

                        def trk(e):
                            ins = None
                            for g in range(4):
                                ins = e.transpose(out=pbt[:, g * 128:(g + 1) * 128], in_=KBD[:, g, :], identity=IDB[:])
                            return ins
                        P.op("pe", trk, reads=[("KBD", tix % 2), "IDB"], writes=[("PS", bt)])
                        pv4 = pbt[:, 0:512].rearrange("p (g t) -> p g t", g=4)
                        P.op("act", lambda e: e.copy(out=KTP[0][0:64, :, kt * 128:(kt + 1) * 128], in_=pv4[0:64]), reads=[("PS", bt)], writes=[("KT", kt)])
                        P.op("act", lambda e: e.copy(out=KTP[1][64:128, :, kt * 128:(kt + 1) * 128], in_=pv4[64:128]), reads=[("PS", bt)], writes=[("KT", kt)])
                    kmm(0)
                    for tix in range(len(tiles_kv)):
                        if tix + 1 < len(tiles_kv):
                            kmm(tix + 1)
                        ktr(tix)
                    for s in range(4):
                        KBD = KBDS[s % 2]
                        KBD4 = KBD[:].rearrange("p g (a d) -> p g a d", a=2)
                        kbk = ("KBD", s % 2)
                        o_toks.append(P.op("sp", lambda e, s=s: e.dma_start(out=o_nks[s, :, :], in_=KS[:, s, :]), reads=[("KS", s)], dma=True))
                        src4 = KS[:, s, :].rearrange("p (h d) -> p h d", h=4)
                        P.op("pool", lambda e, src4=src4: e.tensor_copy(out=KBD4, in_=bcast(src4, [128, 4, 2, 64], 2)), reads=[("KS", s)], writes=[kbk])
                        bt = next_bank()
                        pbt = PS[bt][:].bitcast(BF16)

                        def trks(e, pbt=pbt):
                            ins = None
                            for g in range(4):
                                ins = e.transpose(out=pbt[:, g * 128:(g + 1) * 128], in_=KBD[:, g, :], identity=IDB[:])
                            return ins
                        P.op("pe", trks, reads=[kbk, "IDB"], writes=[("PS", bt)])
                        P.op("act", lambda e, pbt=pbt, s=s: e.copy(out=KTS[:, s, :, :], in_=pbt[:, 0:512].rearrange("p (g t) -> p g t", g=4)),
                             reads=[("PS", bt)], writes=[("KTS", s)])
                    P.barrier()

                if ckpt('kv'):

                    return
                with ExitStack() as sbl:
                    QT2 = [sb("QT2_%d" % i, [128, 4, 128], BF16, sbl) for i in range(2)]
                    QB = [sb("QB%d" % i, [128, 256], BF16, sbl) for i in range(2)]
                    ET = [HT[:, 4 * i:4 * i + 4, H0:H0 + 128] for i in range(4)] + [XT[:, 4 * i:4 * i + 4, H0:H0 + 128] for i in range(4)]
                    T1 = X32T[:, 0:4, H0:H0 + 128]
                    ESK = sb("ESK", [128, 32], F32, sbl)
                    ESK2 = sb("ESK2", [128, 4, 4], F32, sbl)
                    qtiles = [(S0, S0 + 4, 0)] + [(M0 + 128 * i, M0 + 128 * (i + 1), i + 2) for i in range(NMT)]
                    for j in range(2):
                        l = 2 + j
                        blocks = BLK23
                        P.op("sp", lambda e: e.dma_start(out=ESK[:], in_=sinks_b[j, :].partition_broadcast(128)), writes=["ESK"], dma=True)
                        P.op("act", lambda e: e.activation(out=ESK[:], in_=ESK[:], func=AF.Exp), writes=["ESK"])
                        ev = ESK[:].rearrange("p (g i two) -> p g i two", g=4, two=2)
                        P.op("dve", lambda e: e.tensor_copy(out=ESK2[0:64, :, :], in_=ev[0:64, :, :, 0]), reads=["ESK"], writes=["ESK2"])
                        P.op("dve", lambda e: e.tensor_copy(out=ESK2[64:128, :, :], in_=ev[64:128, :, :, 1]), reads=["ESK"], writes=["ESK2"])
                        gate_phase(blocks, True)
                        if ckpt('bgate%d' % l):
                            return
                        NTL = len(qtiles)
                        wqs = {}
                        qtr_done = set()

                        def qmm(gt):
                            g, ti = divmod(gt, NTL)
                            if g not in wqs:
                                wqs[g] = [wnext(), wnext()]
                            wq = wqs[g]
                            (a, b_, csidx) = qtiles[ti]
                            n = b_ - a
                            bk = blk_of(a)
                            for i in range(2):
                                wv, wkey = wq[i]
                                bank = next_bank()

                                def mmq(e, bank=bank, wv=wv):
                                    ins = None
                                    for k in range(NCH):
                                        ins = e.matmul(PS[bank][0:n, 0:256], lhsT=XT[:, k, a:b_], rhs=wv[:, k, :], start=(k == 0), stop=(k == NCH - 1))
                                    return ins
                                P.op("pe", mmq, reads=[wkey] + keys("XT", allch, bk), writes=[("PS", bank)])
                                qb = QB[i]
                                qbk = ("QB", i)
                                src3 = PS[bank][0:n, 0:256].rearrange("p (h d) -> p h d", h=4)
                                dst3 = qb[0:n, :].rearrange("p (h d) -> p h d", h=4)
                                rope(src3, dst3, n, csidx, 4, ("PS", bank), qbk, i)

                        def qtr(gt):
                            g, ti = divmod(gt, NTL)
                            (a, b_, csidx) = qtiles[ti]
                            n = b_ - a
                            qt = QT2[gt % 2]
                            for i in range(2):
                                qb = QB[i]
                                qbk = ("QB", i)
                                bt = next_bank()
                                pbt = PS[bt][:].bitcast(BF16)

                                def trq(e, pbt=pbt, qb=qb):
                                    ins = None
                                    for cc in range(2):
                                        ins = e.transpose(out=pbt[:, cc * 128:cc * 128 + n], in_=qb[0:n, 128 * cc:128 * (cc + 1)], identity=IDB[0:n, 0:n])
                                    return ins
                                P.op("pe", trq, reads=[qbk, "IDB"], writes=[("PS", bt)])
                                P.op("act", lambda e, pbt=pbt, i=i: e.copy(out=qt[:, 2 * i:2 * i + 2, 0:n],
                                                                          in_=pbt[:, 0:256].rearrange("p (c t) -> p c t", c=2)[:, :, 0:n]),
                                     reads=[("PS", bt)], writes=[("QT2", gt % 2, i)])
                            qtr_done.add(gt)

                        units = []
                        for g in range(4):
                            units.append((g * NTL, 0, 4, [(KTS[:, s_, g, :], VSB[:, s_, 64 * g:64 * (g + 1)], "S", ("KTS", s_), ("VSB", s_)) for s_ in range(4)], S0, 0, g))
                            for qb_ in range(NMT):
                                kl = []
                                for kt, mi in ((qb_, 2 if qb_ == 0 else 0), (qb_ + 1, 1)):
                                    kl.append(((KTP[0][:, g, kt * 128:(kt + 1) * 128], KTP[1][:, g, kt * 128:(kt + 1) * 128]), VB[:, kt, 64 * g:64 * (g + 1)], mi, ("KT", kt), ("VB", kt)))
                                units.append((g * NTL + qb_ + 1, 0, 128, kl, M0 + 128 * qb_, 1 if qb_ < 4 else 2, g))

                        def stageA(ui):
                            (gt, off, nq, kl, ha, hb, g) = units[ui]
                            assert gt in qtr_done, (gt, ui)
                            N = 4 * nq
                            qt = QT2[gt % 2]
                            qkeys = [("QT2", gt % 2, 0), ("QT2", gt % 2, 1)]
                            banks = []
                            ets = []
                            if kl[0][2] == "S":
                                bks = [next_bank(), next_bank()]

                                def mms_s(e):
                                    ins = None
                                    for s_, (kTap, vap, mi, kkey, vkey) in enumerate(kl):
                                        for par in range(2):
                                            p0, p1 = 64 * par, 64 * par + 64
                                            ins = e.matmul(PS[bks[par]][:, 0:16].rearrange("p (c s) -> p c s", s=4)[:, :, s_],
                                                           lhsT=kTap[p0:p1, :], rhs=qt[p0:p1, :, s_:s_ + 1], start=True, stop=True)
                                    return ins
                                P.op("pe", mms_s, reads=[x[3] for x in kl] + qkeys, writes=[("PS", bks[0]), ("PS", bks[1])])
                                for par in range(2):
                                    eti = (ui % 2) * 4 + par
                                    et = ET[eti]
                                    ek = ("ET", eti)
                                    P.op("act", lambda e, et=et, par=par: e.activation(out=et[:, :, 0:4], in_=PS[bks[par]][:, 0:16].rearrange("p (c t) -> p c t", c=4),
                                                                                   func=AF.Exp, scale=0.125),
                                         reads=[("PS", bks[par])], writes=[ek])
                                    ets.append((et, ek, None, None, par))
                                return ets
                            for ki, (kTap, vap, mi, kkey, vkey) in enumerate(kl):
                                for par in range(2):
                                    banks.append((next_bank(), ki, par, kTap, mi))

                            def mms(e):
                                ins = None
                                for (bank, ki, par, kTap, mi) in banks:
                                    p0, p1 = 64 * par, 64 * par + 64
                                    if isinstance(kTap, tuple):
                                        ins = e.matmul(PS[bank][:, 0:N], lhsT=kTap[par], rhs=qt[:, :, off:off + nq], start=True, stop=(mi is None))
                                    else:
                                        ins = e.matmul(PS[bank][:, 0:N], lhsT=kTap[p0:p1, :], rhs=qt[p0:p1, :, off:off + nq], start=True, stop=(mi is None))
                                for (bank, ki, par, kTap, mi) in banks:
                                    if mi is not None:
                                        ins = e.matmul(PS[bank][:, 0:N], lhsT=IDB[:], rhs=MASK[:, mi, 0:N], start=False, stop=True)
                                return ins
                            P.op("pe", mms, reads=[x[3] for x in kl] + ["MASK", "IDB"] + qkeys, writes=[("PS", bk_[0]) for bk_ in banks])
                            for (bank, ki, par, kTap, mi) in banks:
                                eti = (ui % 2) * 4 + ki * 2 + par
                                et = ET[eti]
                                ek = ("ET", eti)
                                P.op("act", lambda e, et=et, bank=bank: e.activation(out=et[:, :, 0:nq], in_=PS[bank][:, 0:N].rearrange("p (c t) -> p c t", c=4),
                                                                                 func=AF.Exp, scale=0.125),
                                     reads=[("PS", bank)], writes=[ek])
                                ets.append((et, ek, kl[ki][1], kl[ki][4], par))
                            return ets

                        def stageB(ui, ets):
                            (gt, off, nq, kl, ha, hb, g) = units[ui]
                            N = 4 * nq
                            bo = next_bank()
                            bd = next_bank()

                            def mmo_s(e):
                                ins = None
                                for par in range(2):
                                    p0, p1 = 64 * par, 64 * par + 64
                                    et = [x for x in ets if x[4] == par][0][0]
                                    for s_, (kTap, vap, mi, kkey, vkey) in enumerate(kl):
                                        ins = e.matmul(PS[bo][p0:p1, 0:16].rearrange("p (c s) -> p c s", s=4)[:, :, s_], lhsT=vap, rhs=et[:, :, s_:s_ + 1], start=True, stop=True)
                                    for s_, (kTap, vap, mi, kkey, vkey) in enumerate(kl):
                                        ins = e.matmul(PS[bd][p0:p1, 0:16].rearrange("p (c s) -> p c s", s=4)[:, :, s_], lhsT=ONES64[:], rhs=et[:, :, s_:s_ + 1], start=True, stop=True)
                                return ins

                            def mmo(e):
                                ins = None
                                for par in range(2):
                                    p0, p1 = 64 * par, 64 * par + 64
                                    sel = [x for x in ets if x[4] == par]
                                    for ii, (et, ek, vap, vkey, _) in enumerate(sel):
                                        ins = e.matmul(PS[bo][p0:p1, 0:N], lhsT=vap, rhs=et[:, :, 0:nq], start=(ii == 0), stop=(ii == len(sel) - 1))
                                    for ii, (et, ek, vap, vkey, _) in enumerate(sel):
                                        ins = e.matmul(PS[bd][p0:p1, 0:N], lhsT=ONES64[:], rhs=et[:, :, 0:nq], start=(ii == 0), stop=(ii == len(sel) - 1))
                                return ins
                            if kl[0][2] == "S":
                                P.op("pe", mmo_s, reads=[x[1] for x in ets] + [x[4] for x in kl] + ["ONES64"], writes=[("PS", bo), ("PS", bd)])
                            else:
                                P.op("pe", mmo, reads=[x[1] for x in ets] + [x[3] for x in ets] + ["ONES64"], writes=[("PS", bo), ("PS", bd)])
                            R = RR[ui % 2]
                            rk = ("RR", ui % 2)
                            P.op("dve", lambda e: e.tensor_tensor(out=R[:, 0:N].rearrange("p (c t) -> p c t", c=4),
                                                                  in0=PS[bd][:, 0:N].rearrange("p (c t) -> p c t", c=4),
                                                                  in1=bcast(ESK2[:, g, :], [128, 4, nq], 2), op=ALU.add),
                                 reads=[("PS", bd), "ESK2"], writes=[rk])
                            P.op("act", lambda e: e.activation(out=R[:, 0:N], in_=R[:, 0:N], func=AF.Ln), writes=[rk])
                            P.op("act", lambda e: e.activation(out=R[:, 0:N], in_=R[:, 0:N], func=AF.Exp, scale=-1.0), writes=[rk])
                            P.op("dve", lambda e: e.tensor_tensor(out=T1[:, :, 0:nq], in0=PS[bo][:, 0:N].rearrange("p (c t) -> p c t", c=4),
                                                                  in1=R[:, 0:N].rearrange("p (c t) -> p c t", c=4), op=ALU.mult),
                                 reads=[("PS", bo), rk], writes=["T1"])
                            hk = keys("HT", range(4 * g, 4 * g + 4), hb)
                            P.op("dve", lambda e: e.tensor_tensor(out=HT[:, 4 * g:4 * g + 4, ha:ha + nq],
                                                                  in0=T1[:, :, 0:nq],
                                                                  in1=HT[:, 4 * g:4 * g + 4, ha:ha + nq], op=ALU.mult),
                                 reads=["T1"], writes=hk)

                        TT = 4 * NTL
                        last_unit = {}
                        for ui_, u_ in enumerate(units):
                            last_unit[u_[0]] = ui_
                        qmm(0)
                        qtr(0)
                        qmm(1)
                        qtr(1)
                        pend = stageA(0)
                        qmm(2)
                        pend_tr = 2
                        nextq = 3
                        for ui in range(len(units)):
                            if ui + 1 < len(units) and units[ui + 1][0] not in qtr_done:
                                assert pend_tr == units[ui + 1][0], (pend_tr, units[ui + 1][0])
                                qtr(pend_tr)
                                pend_tr = None
                            nxt = stageA(ui + 1) if ui + 1 < len(units) else None
                            if pend_tr is not None and last_unit[pend_tr - 2] <= ui + 1:
                                qtr(pend_tr)
                                pend_tr = None
                            stageB(ui, pend)
                            if pend_tr is None and nextq < TT:
                                qmm(nextq)
                                pend_tr = nextq
                                nextq += 1
                            pend = nxt
                        assert nextq == TT and pend_tr is None, (nextq, pend_tr)
                        if ckpt('attn%d' % l):
                            return
                        wout_phase(l, blocks)
                        dump(l)
                        if ckpt('L%d' % l):
                            return
                    P.barrier()

            with ExitStack() as so:
                OST = [sb("OST%d" % i, [128, D], F32, so) for i in range(2)]
                units = [(S0, S0 + 4, None)] + [(M0 + 128 * i, M0 + 128 * (i + 1), i) for i in range(NMT)]
                for ui, (a, b_, mt) in enumerate(units):
                    n = b_ - a
                    ost = OST[ui % 2]
                    okey = ("OST", ui % 2)
                    for cg in range(4):
                        bank = next_bank()

                        def tro(e, cg=cg, bank=bank, a=a, b_=b_, n=n):
                            ins = None
                            for cc in range(4):
                                c = cg * 4 + cc
                                ins = e.transpose(out=PS[bank][0:n, cc * 128:(cc + 1) * 128], in_=X32T[:, c, a:b_], identity=IDF[:])
                            return ins
                        P.op("pe", tro, reads=["IDF"], writes=[("PS", bank)])
                        eng = "dve" if cg % 2 == 0 else "act"
                        if eng == "dve":
                            P.op("dve", lambda e, cg=cg, bank=bank, n=n, ost=ost: e.tensor_copy(out=ost[0:n, cg * 512:(cg + 1) * 512], in_=PS[bank][0:n, :]),
                                 reads=[("PS", bank)], writes=[okey + (cg,)])
                        else:
                            P.op("act", lambda e, cg=cg, bank=bank, n=n, ost=ost: e.copy(out=ost[0:n, cg * 512:(cg + 1) * 512], in_=PS[bank][0:n, :]),
                                 reads=[("PS", bank)], writes=[okey + (cg,)])
                    rk = [okey + (cg,) for cg in range(4)]
                    if mt is None:
                        o_toks.append(P.op("sp", lambda e, ost=ost: e.dma_start(out=o_ys[:, :], in_=ost[0:4, :]), reads=rk, dma=True))
                    else:
                        o_toks.append(P.op("sp", lambda e, ost=ost, mt=mt: e.dma_start(out=o_yp[128 * mt:128 * (mt + 1), :], in_=ost[:, :]), reads=rk, dma=True))
        if stage != 'const':
            phases()
        if stage is not None:
            P.barrier()
            if debug:
                P.op('sp', lambda e: e.dma_start(out=o_dbg[3, :, :, :], in_=X32T[:]), dma=True)
        P.final_wait("sp")
    return nc


_NC_CACHE = {}


def host_consts(core):
    b, h = core // 2, core % 2
    start = 1024 * h
    inv = (np.float32(500000.0) ** (-np.arange(0, 16, 2, dtype=np.float32) / np.float32(16.0))).astype(np.float32)
    pos = np.zeros((128, 10), np.float32)
    pos[:, 0] = PAST
    for t in range(9):
        pos[:, t + 1] = start - 128 + 128 * t + np.arange(128)
    ang = (pos[:, :, None].astype(np.float32) * inv[None, None, :]).astype(np.float32)
    cs = np.concatenate([np.cos(ang), np.sin(ang)], axis=-1).astype(np.float32)
    k = np.arange(128)[:, None]
    q = np.arange(128)[None, :]
    mprev = (k > q).astype(np.float32)
    mown = (k <= q).astype(np.float32)
    mask = np.stack([mprev, mown, mprev * float(h)], axis=1)
    mask = np.tile((mask - 1.0) * 30000.0, (1, 1, 4)).astype(ml_dtypes.bfloat16)
    hm = np.full((128, 1), float(h), np.float32)
    pw = np.zeros((60, 4, 4), np.float32)
    for g, w in enumerate(WINS):
        for s in range(4):
            for jj in range(15 - (w - 1), 15):
                pw[15 * s + jj, g, s] = 1.0 / w
    rfix = np.zeros((128, 4, 15), np.float32)
    for g, w in enumerate(WINS):
        for t in range(15):
            cnt = min(w, t + 1) if h == 0 else w
            rfix[:, g, t] = 1.0 / cnt
    return cs, mask, hm, pw, rfix


def kernel(x_prompt, x_sample, state_pool, cache_k, cache_v, w_in_a, w_grp_a, scale_a, w_out_a,
           w_kv, w_in_b, sinks_b, w_out_b, ln_g, ln_b, _debug=False, _stage=None, _cores=None):
    f = lambda a: np.ascontiguousarray(np.asarray(a, dtype=np.float32))
    x_prompt, x_sample, state_pool, cache_k, cache_v = map(f, (x_prompt, x_sample, state_pool, cache_k, cache_v))
    shared = {"w_in_a": f(w_in_a), "w_grp_a": f(w_grp_a), "scale_a": f(scale_a), "w_out_a": f(w_out_a), "w_kv": f(w_kv),
              "w_in_b": f(w_in_b), "sinks_b": f(sinks_b), "w_out_b": f(w_out_b), "ln_g": f(ln_g), "ln_b": f(ln_b)}
    key = (bool(_debug), _stage)
    if key not in _NC_CACHE:
        _NC_CACHE[key] = build(debug=bool(_debug), stage=_stage)
    nc = _NC_CACHE[key]
    in_maps = []
    for core in range(8):
        b, h = core // 2, core % 2
        start = 1024 * h
        xin = np.zeros((NT, D), np.float32)
        xin[0:4] = x_sample[4 * core:4 * core + 4, 0]
        if h == 1:
            xin[E0:M0] = x_prompt[b, start - 160:start]
        xin[M0:] = x_prompt[b, start:start + 1024]
        cs, mask, hm, pw, rfix = host_consts(core)
        m = dict(shared)
        m.update({"xin": xin,
                  "stp": np.ascontiguousarray(state_pool[:, 4 * core:4 * core + 4].reshape(2, 60, D)),
                  "ck": np.ascontiguousarray(cache_k[4 * core:4 * core + 4].reshape(4, 128, 256)),
                  "cv": np.ascontiguousarray(cache_v[4 * core:4 * core + 4].reshape(4, 128, 256)),
                  "c_cs": cs, "c_mask": mask, "c_hm": hm, "c_pw": pw, "c_rfix": rfix})
        in_maps.append(m)
    if _cores is not None:
        res = run_bass_kernel_spmd(nc, [in_maps[c] for c in _cores], core_ids=list(range(len(_cores))))
        return res.results
    res = run_bass_kernel_spmd(nc, in_maps, core_ids=list(range(8)))
    R = res.results
    y_prompt = np.zeros((4, 2048, D), np.float32)
    y_sample = np.zeros((32, 1, D), np.float32)
    npp = np.zeros((2, 4, 15, D), np.float32)
    nps = np.zeros((2, 32, 15, D), np.float32)
    nkp = np.zeros((4, 128, 4, 64), np.float32)
    nvp = np.zeros((4, 128, 4, 64), np.float32)
    nks = np.zeros((32, 128, 4, 64), np.float32)
    nvs = np.zeros((32, 128, 4, 64), np.float32)
    for core in range(8):
        b, h = core // 2, core % 2
        r = R[core]
        y_prompt[b, 1024 * h:1024 * (h + 1)] = r["o_yp"]
        y_sample[4 * core:4 * core + 4, 0] = r["o_ys"]
        nps[:, 4 * core:4 * core + 4] = r["o_nps"]
        nks[4 * core:4 * core + 4] = r["o_nks"].reshape(4, 128, 4, 64)
        nvs[4 * core:4 * core + 4] = r["o_nvs"].reshape(4, 128, 4, 64)
        if h == 1:
            npp[:, b] = r["o_npp"]
            nkp[b] = r["o_nkp"].reshape(128, 4, 64)
            nvp[b] = r["o_nvp"].reshape(128, 4, 64)
    if _debug:
        return (y_prompt, y_sample, npp, nps, nkp, nvp, nks, nvs), [r["o_dbg"] for r in R]
    return (y_prompt, y_sample, npp, nps, nkp, nvp, nks, nvs)
```

```python
import numpy as np
import ml_dtypes
import concourse.bass as bass
import concourse.mybir as mybir
from concourse.bass_utils import run_bass_kernel_spmd
from contextlib import ExitStack

F32 = mybir.dt.float32
BF16 = mybir.dt.bfloat16
ALU = mybir.AluOpType
AF = mybir.ActivationFunctionType

D = 2048
NCH = 16
NT = 1188
S0, E0, H0, M0 = 0, 4, 36, 164
NMT = 8
BLK01 = [(0, 164), (164, 676), (676, 1188)]
BLK23 = [(0, 4), (164, 676), (676, 1188)]
ALPHA = 8.0 ** 0.25
LN_EPS = 1e-5
WINS = (2, 4, 8, 16)
PAST = 16384
DEBUG = False


def blk_of(c0):
    return 0 if c0 < 164 else (1 if c0 < 676 else 2)


class Tok:
    __slots__ = ("sem", "val", "eng", "dma")

    def __init__(self, sem, val, eng, dma):
        self.sem, self.val, self.eng, self.dma = sem, val, eng, dma


class Prog:
    KQ = 12

    def __init__(self, nc, stack):
        self.nc = nc
        self.h = {"pe": nc.tensor, "act": nc.scalar, "dve": nc.vector, "pool": nc.gpsimd, "sp": nc.sync}
        self.sem = {e: stack.enter_context(nc.semaphore("s_" + e)) for e in self.h}
        self.cnt = {e: 0 for e in self.h}
        self.waited = {e: {} for e in self.h}
        self.lastw = {}
        self.readers = {}
        self.dq = {}
        for q in ("sp", "pool", "act"):
            self.dq[q] = {"sems": [stack.enter_context(nc.semaphore("d_%s%d" % (q, i))) for i in range(self.KQ)], "n": 0}
        self.semid = {}
        self.all_toks = []
        self.out_toks = []

    def _wait(self, e, sem, val):
        key = id(sem)
        self.semid[key] = sem
        if self.waited[e].get(key, 0) < val:
            self.h[e].wait_ge(sem, val)
            self.waited[e][key] = val

    def op(self, e, fn, reads=(), writes=(), dma=False, extra=()):
        deps = list(extra)
        psr = [k for k in reads if isinstance(k, tuple) and k[0] == "PS"]
        if psr:
            reads = [k for k in reads if k not in psr]
            writes = list(writes) + psr
        for k in reads:
            t = self.lastw.get(k)
            if t is not None:
                deps.append(t)
        for k in writes:
            t = self.lastw.get(k)
            if t is not None:
                deps.append(t)
            deps.extend(self.readers.get(k, ()))
        need = {}
        for t in deps:
            if e == "pe" and (not dma) and t.eng == "pe" and (not t.dma):
                continue
            key = id(t.sem)
            self.semid[key] = t.sem
            if need.get(key, 0) < t.val:
                need[key] = t.val
        for key, val in need.items():
            self._wait(e, self.semid[key], val)
        if dma:
            q = self.dq[e]
            j = q["n"]
            q["n"] += 1
            sem = q["sems"][j % self.KQ]
            if j >= self.KQ:
                self._wait(e, sem, 16 * (j // self.KQ))
            ins = fn(self.h[e])
            val = 16 * (j // self.KQ + 1)
            ins.then_inc(sem, 16)
            tok = Tok(sem, val, e, True)
        else:
            ins = fn(self.h[e])
            self.cnt[e] += 1
            ins.then_inc(self.sem[e], 1)
            tok = Tok(self.sem[e], self.cnt[e], e, False)
        for k in reads:
            self.readers.setdefault(k, []).append(tok)
        for k in writes:
            self.lastw[k] = tok
            self.readers[k] = []
        self.all_toks.append(tok)
        if len(self.all_toks) > 4096:
            self._compact()
        return tok

    def _compact(self):
        best = {}
        for t in self.all_toks:
            k = id(t.sem)
            if k not in best or best[k].val < t.val:
                best[k] = t
        self.all_toks = list(best.values())

    def barrier(self):
        self._compact()
        toks = list(self.all_toks)
        for e in self.h:
            for t in toks:
                self._wait(e, t.sem, t.val)
        self.lastw = {}
        self.readers = {}

    def final_wait(self, e="sp"):
        self._compact()
        for t in self.all_toks:
            self._wait(e, t.sem, t.val)


def bcast(ap, shape, axis):
    return ap.unsqueeze(axis).broadcast_to(shape)


class _Stop(Exception):
    pass


def build(debug=False, stage=None):
    nc = bass.Bass("TRN2", target_bir_lowering=False, dynamic_dma_scratch_size=4096)

    def din(name, shape, dt=F32):
        return nc.dram_tensor(name, list(shape), dt, kind="ExternalInput").ap()

    def dout(name, shape, dt=F32):
        return nc.dram_tensor(name, list(shape), dt, kind="ExternalOutput").ap()

    xin = din("xin", [NT, D])
    stp = din("stp", [2, 60, D])
    ck = din("ck", [4, 128, 256])
    cv = din("cv", [4, 128, 256])
    w_in_a = din("w_in_a", [2, D, 2 * D])
    w_grp_a = din("w_grp_a", [2, 4, 512, 512])
    scale_a = din("scale_a", [2, D])
    w_out_a = din("w_out_a", [2, D, D])
    w_kv = din("w_kv", [D, 512])
    w_in_b = din("w_in_b", [2, D, 2 * D])
    sinks_b = din("sinks_b", [2, 32])
    w_out_b = din("w_out_b", [2, D, D])
    ln_g = din("ln_g", [4, D])
    ln_b = din("ln_b", [4, D])
    c_cs = din("c_cs", [128, 10, 16])
    c_mask = din("c_mask", [128, 3, 512], BF16)
    c_hm = din("c_hm", [128, 1])
    c_pw = din("c_pw", [60, 4, 4])
    c_rfix = din("c_rfix", [128, 4, 15])

    o_yp = dout("o_yp", [1024, D])
    o_ys = dout("o_ys", [4, D])
    o_npp = dout("o_npp", [2, 15, D])
    o_nps = dout("o_nps", [2, 4, 15, D])
    o_nkp = dout("o_nkp", [128, 256])
    o_nvp = dout("o_nvp", [128, 256])
    o_nks = dout("o_nks", [4, 128, 256])
    o_nvs = dout("o_nvs", [4, 128, 256])
    o_dbg = dout("o_dbg", [4, 128, NCH, NT]) if debug else None

    with ExitStack() as st:
        P = Prog(nc, st)

        def sb(name, shape, dt=F32, stack=st):
            return stack.enter_context(nc.sbuf_tensor(name, list(shape), dt))

        X32T = sb("X32T", [128, NCH, NT])
        XT = sb("XT", [128, NCH, NT], BF16)
        HT = sb("HT", [128, NCH, NT], BF16)
        WB = [sb("WB%d" % i, [128, 4096], BF16) for i in range(3)]
        IDB = sb("IDB", [128, 128], BF16)
        IDF = sb("IDF", [128, 128])
        ONESM = sb("ONESM", [128, 128], BF16)
        ONES64 = sb("ONES64", [128, 64], BF16)
        EPS = sb("EPS", [128, 1])
        HM = sb("HM", [128, 1])
        GAM = sb("GAM", [128, 4, NCH])
        BET = sb("BET", [128, 4, NCH])
        SCA = sb("SCA", [128, 2, NCH])
        CS = sb("CS", [128, 10, 16])
        MASK = sb("MASK", [128, 3, 512], BF16)
        PW = sb("PW", [60, 4, 4])
        RFIX = sb("RFIX", [128, 4, 15])
        RR = [sb("RR%d" % i, [128, 512]) for i in range(2)]
        PS = [st.enter_context(nc.psum_tensor("ps%d" % i, [128, 512], F32)) for i in range(8)]
        bank_ctr = [0]

        def next_bank():
            b = bank_ctr[0] % 8
            bank_ctr[0] += 1
            return b

        P.op("sp", lambda e: e.dma_start(out=HM[:], in_=c_hm[:, :]), writes=["HM"], dma=True)
        P.op("sp", lambda e: e.dma_start(out=CS[:], in_=c_cs[:, :, :]), writes=["CS"], dma=True)
        P.op("sp", lambda e: e.dma_start(out=MASK[:], in_=c_mask[:, :, :]), writes=["MASK"], dma=True)
        P.op("sp", lambda e: e.dma_start(out=PW[:], in_=c_pw[:, :, :]), writes=["PW"], dma=True)
        P.op("sp", lambda e: e.dma_start(out=RFIX[:], in_=c_rfix[:, :, :]), writes=["RFIX"], dma=True)
        P.op("pool", lambda e: e.memset(IDF[:], 0.0), writes=["IDF"])
        P.op("pool", lambda e: e.iota(IDF[:], [[1, 128]], base=0, channel_multiplier=-1, allow_small_or_imprecise_dtypes=True), writes=["IDF"])
        P.op("dve", lambda e: e.tensor_scalar(out=IDF[:], in0=IDF[:], scalar1=0.0, scalar2=None, op0=ALU.is_equal), writes=["IDF"])
        P.op("dve", lambda e: e.tensor_copy(out=IDB[:], in_=IDF[:]), reads=["IDF"], writes=["IDB"])
        P.op("dve", lambda e: e.memset(ONESM[:], 1.0 / D), writes=["ONESM"])
        with ExitStack() as sprm:
            PRM = sb("PRM", [64, 3, 128], F32, sprm)
            P.op("sp", lambda e: e.dma_start(out=PRM[0:64, 0, :], in_=ln_g.rearrange("l (c p) -> (l c) p", p=128)), writes=["PRM0"], dma=True)
            P.op("sp", lambda e: e.dma_start(out=PRM[0:64, 1, :], in_=ln_b.rearrange("l (c p) -> (l c) p", p=128)), writes=["PRM1"], dma=True)
            P.op("sp", lambda e: e.dma_start(out=PRM[0:32, 2, :], in_=scale_a.rearrange("l (c p) -> (l c) p", p=128)), writes=["PRM2"], dma=True)

            def trp(e):
                e.transpose(out=PS[0][:, 0:64], in_=PRM[0:64, 0, :], identity=IDF[0:64, 0:64])
                e.transpose(out=PS[0][:, 64:128], in_=PRM[0:64, 1, :], identity=IDF[0:64, 0:64])
                return e.transpose(out=PS[0][:, 128:160], in_=PRM[0:32, 2, :], identity=IDF[0:32, 0:32])
            P.op("pe", trp, reads=["PRM0", "PRM1", "PRM2", "IDF"], writes=[("PS", 0)])
            P.op("dve", lambda e: e.tensor_copy(out=GAM[:].rearrange("p l c -> p (l c)"), in_=PS[0][:, 0:64]), reads=[("PS", 0)], writes=["GAM"])
            P.op("dve", lambda e: e.tensor_copy(out=BET[:].rearrange("p l c -> p (l c)"), in_=PS[0][:, 64:128]), reads=[("PS", 0)], writes=["BET"])
            P.op("dve", lambda e: e.tensor_copy(out=SCA[:].rearrange("p l c -> p (l c)"), in_=PS[0][:, 128:160]), reads=[("PS", 0)], writes=["SCA"])
            P.barrier()
        P.op("dve", lambda e: e.memset(ONES64[:], 1.0), writes=["ONES64"])
        P.op("dve", lambda e: e.memset(EPS[:], LN_EPS), writes=["EPS"])

        def wview(buf, nchunk, ncol):
            return WB[buf][:, 0:nchunk * ncol].rearrange("p (c f) -> p c f", c=nchunk)

        wstate = {"i": 0, "pending": None}
        wlist = []

        def wissue(idx):
            ap2d, nchunk, ncol = wlist[idx]
            buf = idx % 3
            dst = wview(buf, nchunk, ncol)
            src = ap2d.rearrange("(c p) f -> p c f", p=128)
            P.op("pool", lambda e: e.dma_start(out=dst, in_=src), writes=[("W", buf)], dma=True)

        wstate["issued"] = -1

        def wensure(upto):
            while wstate["issued"] < min(upto, len(wlist) - 1):
                wstate["issued"] += 1
                wissue(wstate["issued"])

        def wnext(prefetch=True):
            i = wstate["i"]
            wensure(i + 1 if prefetch else i)
            wstate["i"] = i + 1
            ap2d, nchunk, ncol = wlist[i]
            return wview(i % 3, nchunk, ncol), ("W", i % 3)

        def wprefetch():
            wensure(wstate["issued"] + 1)

        for l in range(2):
            for j in range(8):
                wlist.append((w_in_a[l, :, D + 256 * j:D + 256 * (j + 1)], 16, 256))
            for g in range(4):
                for i in range(2):
                    wlist.append((w_in_a[l, :, 512 * g + 256 * i:512 * g + 256 * (i + 1)], 16, 256))
                wlist.append((w_grp_a[l, g, :, :], 4, 512))
            for j in range(8):
                wlist.append((w_out_a[l, :, 256 * j:256 * (j + 1)], 16, 256))
        wlist.append((w_kv[:, 256:512], 16, 256))
        wlist.append((w_kv[:, 0:256], 16, 256))
        for l in range(2):
            for j in range(8):
                wlist.append((w_in_b[l, :, D + 256 * j:D + 256 * (j + 1)], 16, 256))
            for j in range(8):
                wlist.append((w_in_b[l, :, 256 * j:256 * (j + 1)], 16, 256))
            for j in range(8):
                wlist.append((w_out_b[l, :, 256 * j:256 * (j + 1)], 16, 256))

        allch = list(range(NCH))

        def keys(name, chs, b):
            return [(name, c, b) for c in chs]

        def modeF_chunk(wv, wkey, fsub, nK, rhs_of, rhs_keys_of, blocks, evac):
            for (c0, c1) in blocks:
                n = c1 - c0
                b = blk_of(c0)
                bank = next_bank()

                def mm(e, c0=c0, c1=c1, n=n, bank=bank):
                    ins = None
                    for k in range(nK):
                        ins = e.matmul(PS[bank][:, 0:n], lhsT=wv[:, k, fsub * 128:(fsub + 1) * 128], rhs=rhs_of(k, c0, c1),
                                       start=(k == 0), stop=(k == nK - 1))
                    return ins
                P.op("pe", mm, reads=[wkey] + rhs_keys_of(b), writes=[("PS", bank)])
                evac(bank, c0, c1, b)

        def layer_norm(l, blocks, between=None):
            nb = len(blocks)
            info = [(c0, c1, c1 - c0, blk_of(c0)) for (c0, c1) in blocks]
            bms = []
            for (c0, c1, n, b) in info:
                bm = next_bank()
                bms.append(bm)

                def mm_mu(e, bm=bm, c0=c0, c1=c1, n=n):
                    ins = None
                    for k in range(NCH):
                        ins = e.matmul(PS[bm][:, 0:n], lhsT=ONESM[:], rhs=XT[:, k, c0:c1], start=(k == 0), stop=(k == NCH - 1))
                    return ins
                P.op("pe", mm_mu, reads=keys("XT", allch, b) + ["ONESM"], writes=[("PS", bm)])
            for (c0, c1, n, b), bm in zip(info, bms):
                P.op("dve", lambda e: e.tensor_tensor(out=X32T[:, :, c0:c1], in0=X32T[:, :, c0:c1],
                                                      in1=bcast(PS[bm][:, 0:n], [128, NCH, n], 1), op=ALU.subtract),
                     reads=[("PS", bm)], writes=keys("X32", allch, b))
            CS_ = 9
            for (c0, c1, n, b) in info:
                P.op("act", lambda e: e.activation(out=XT[:, 0:CS_, c0:c1], in_=X32T[:, 0:CS_, c0:c1], func=AF.Square),
                     reads=keys("X32", range(CS_), b), writes=keys("XT", range(CS_), b))
                P.op("dve", lambda e: e.tensor_tensor(out=XT[:, CS_:NCH, c0:c1], in0=X32T[:, CS_:NCH, c0:c1], in1=X32T[:, CS_:NCH, c0:c1], op=ALU.mult),
                     reads=keys("X32", range(CS_, NCH), b), writes=keys("XT", range(CS_, NCH), b))
            bvs = []
            for bix, (c0, c1, n, b) in enumerate(info):
                bv = (5 + bix) if between is not None else next_bank()
                bvs.append(bv)

                def mm_var(e, bv=bv, c0=c0, c1=c1, n=n):
                    ins = None
                    for k in range(NCH):
                        ins = e.matmul(PS[bv][:, 0:n], lhsT=ONESM[:], rhs=XT[:, k, c0:c1], start=(k == 0), stop=(k == NCH - 1))
                    return ins
                P.op("pe", mm_var, reads=keys("XT", allch, b) + ["ONESM"], writes=[("PS", bv)])
            for bi, ((c0, c1, n, b), bv) in enumerate(zip(info, bvs)):
                R = RR[bi % 2]
                rk = ("RR", bi % 2)
                P.op("act", lambda e: e.activation(out=R[:, 0:n], in_=PS[bv][:, 0:n], func=AF.Sqrt, bias=EPS[:], scale=1.0),
                     reads=[("PS", bv), "EPS"], writes=[rk])
                P.op("dve", lambda e: e.reciprocal(out=R[:, 0:n], in_=R[:, 0:n]), writes=[rk])
                P.op("dve", lambda e: e.tensor_tensor(out=X32T[:, :, c0:c1], in0=X32T[:, :, c0:c1],
                                                      in1=bcast(R[:, 0:n], [128, NCH, n], 1), op=ALU.mult),
                     reads=[rk], writes=keys("X32", allch, b))
                for c in range(NCH):
                    if c < 10:
                        P.op("dve", lambda e, c=c: e.tensor_scalar(out=X32T[:, c, c0:c1], in0=X32T[:, c, c0:c1],
                                                                   scalar1=GAM[:, l, c:c + 1], scalar2=BET[:, l, c:c + 1],
                                                                   op0=ALU.mult, op1=ALU.add),
                             reads=["GAM", "BET"], writes=[("X32", c, b)])
                    else:
                        P.op("act", lambda e, c=c: e.activation(out=X32T[:, c, c0:c1], in_=X32T[:, c, c0:c1], func=AF.Identity,
                                                                scale=GAM[:, l, c:c + 1], bias=BET[:, l, c:c + 1]),
                             reads=["GAM", "BET"], writes=[("X32", c, b)])
                P.op("dve", lambda e: e.tensor_copy(out=XT[:, 0:6, c0:c1], in_=X32T[:, 0:6, c0:c1]),
                     reads=keys("X32", range(6), b), writes=keys("XT", range(6), b))
                P.op("act", lambda e: e.copy(out=XT[:, 6:NCH, c0:c1], in_=X32T[:, 6:NCH, c0:c1]),
                     reads=keys("X32", range(6, NCH), b), writes=keys("XT", range(6, NCH), b))
                if between is not None:
                    between(bi)

        def wout_phase(l, blocks, between=None):
            for j in range(8):
                wv, wkey = wnext()
                for fsub in range(2):
                    c = 2 * j + fsub

                    def evac(bank, c0, c1, b, c=c):
                        n = c1 - c0
                        P.op("dve", lambda e: e.scalar_tensor_tensor(out=X32T[:, c, c0:c1], in0=X32T[:, c, c0:c1], scalar=ALPHA,
                                                                     in1=PS[bank][:, 0:n], op0=ALU.mult, op1=ALU.add),
                             reads=[("PS", bank)], writes=[("X32", c, b)])
                        P.op("act", lambda e: e.copy(out=XT[:, c, c0:c1], in_=X32T[:, c, c0:c1]), reads=[("X32", c, b)], writes=[("XT", c, b)])
                    modeF_chunk(wv, wkey, fsub, NCH, lambda k, c0, c1: HT[:, k, c0:c1], lambda b: keys("HT", allch, b), blocks, evac)
            layer_norm(l, blocks, between)

        def gate_phase(blocks, stagger):
            def piece(wv, wkey, j, bl):
                for fsub in range(2):
                    c = 2 * j + fsub

                    def evac(bank, c0, c1, b, c=c):
                        n = c1 - c0
                        P.op("act", lambda e: e.activation(out=HT[:, c, c0:c1], in_=PS[bank][:, 0:n], func=AF.Silu),
                             reads=[("PS", bank)], writes=[("HT", c, b)])
                    modeF_chunk(wv, wkey, fsub, NCH, lambda k, c0, c1: XT[:, k, c0:c1], lambda b: keys("XT", allch, b), bl, evac)
            j0 = 0
            if stagger:
                ws = [wnext(), wnext(), wnext(prefetch=False)]
                for j in range(3):
                    piece(ws[j][0], ws[j][1], j, blocks[:-1])
                for j in range(3):
                    piece(ws[j][0], ws[j][1], j, blocks[-1:])
                    if j == 0:
                        wprefetch()
                j0 = 3
            for j in range(j0, 8):
                wv, wkey = wnext()
                piece(wv, wkey, j, blocks)

        def dump(idx):
            if debug:
                P.barrier()
                P.op("sp", lambda e: e.dma_start(out=o_dbg[idx, :, :, :], in_=X32T[:]), dma=True)
                P.barrier()

        def ckpt(name):
            return stage == name

        def phases():
            with ExitStack() as s0:
                XIN = [sb("XIN%d" % i, [128, D], F32, s0) for i in range(2)]
                rt = 0
                r0 = 0
                while r0 < NT:
                    r1 = min(r0 + 128, NT)
                    n = r1 - r0
                    xi = XIN[rt % 2]
                    xk = ("XIN", rt % 2)
                    P.op("sp", lambda e, xi=xi, r0=r0, r1=r1, n=n: e.dma_start(out=xi[0:n, :], in_=xin[r0:r1, :]), writes=[xk], dma=True)
                    for cg in range(4):
                        bank = next_bank()

                        def tr(e, xi=xi, n=n, cg=cg, bank=bank):
                            ins = None
                            for cc in range(4):
                                c = cg * 4 + cc
                                ins = e.transpose(out=PS[bank][:, cc * 128:cc * 128 + n], in_=xi[0:n, c * 128:(c + 1) * 128], identity=IDF[0:n, 0:n])
                            return ins
                        P.op("pe", tr, reads=[xk, "IDF"], writes=[("PS", bank)])
                        src = PS[bank][:].rearrange("p (c t) -> p c t", c=4)[:, :, 0:n]
                        P.op("dve", lambda e, src=src, cg=cg, r0=r0, r1=r1: e.tensor_copy(out=X32T[:, cg * 4:cg * 4 + 4, r0:r1], in_=src),
                             reads=[("PS", bank)], writes=[("X32p0", cg, rt)])
                        P.op("act", lambda e, cg=cg, r0=r0, r1=r1: e.copy(out=XT[:, cg * 4:cg * 4 + 4, r0:r1], in_=X32T[:, cg * 4:cg * 4 + 4, r0:r1]),
                             reads=[("X32p0", cg, rt)], writes=[("XTp0", cg, rt)])
                    rt += 1
                    r0 = r1
                P.barrier()
                if ckpt('p0'):
                    return

            o_toks = []
            with ExitStack() as sa:
                US = [sb("U%d" % i, [128, NT], F32, sa) for i in range(2)]
                TA = sb("TA", [128, NT], F32, sa)
                TB = sb("TB", [128, NT], F32, sa)
                DT = sb("DT", [128, 4, NT], BF16, sa)
                STT = sb("STT", [60, 512], F32, sa)
                UKS = sb("UKS", [128, NCH, 20], F32, sa)
                FX = sb("FX", [128, 16], F32, sa)
                PST = sb("PST", [20, 1024], F32, sa)
                P.op("pool", lambda e: e.memset(DT[:], 0.0), writes=keys("DT", range(4), 0) + keys("DT", range(4), 1) + keys("DT", range(4), 2))
                for l in range(2):
                    blocks = BLK01
                    gate_phase(blocks, l >= 1)
                    if ckpt('gate%d' % l):
                        return
                    for g in range(4):
                        w = WINS[g]
                        P.op("sp", lambda e, g=g: e.dma_start(out=STT[:, :], in_=stp[l, :, 512 * g:512 * (g + 1)]), writes=["STT"], dma=True)
                        for i in range(2):
                            wv, wkey = wnext()
                            for fsub in range(2):
                                cc = 2 * i + fsub
                                c = 4 * g + cc
                                U = US[c % 2]
                                up = c % 2

                                def evac(bank, c0, c1, b):
                                    n = c1 - c0
                                    if b == 0:
                                        P.op("act", lambda e: e.copy(out=U[:, 0:4], in_=PS[bank][:, 0:4]), reads=[("PS", bank)], writes=[("U", up, 0)])
                                        P.op("act", lambda e: e.activation(out=U[:, 4:164], in_=PS[bank][:, 4:164], func=AF.Identity, scale=HM[:]),
                                             reads=[("PS", bank), "HM"], writes=[("U", up, 0)])
                                    else:
                                        P.op("act", lambda e: e.copy(out=U[:, c0:c1], in_=PS[bank][:, 0:n]), reads=[("PS", bank)], writes=[("U", up, b)])
                                modeF_chunk(wv, wkey, fsub, NCH, lambda k, c0, c1: XT[:, k, c0:c1], lambda b: keys("XT", allch, b), blocks, evac)
                                ukeys = [("U", up, 0), ("U", up, 1), ("U", up, 2)]
                                P.op("pool", lambda e, c=c: e.tensor_copy(out=UKS[:, c, 0:4], in_=U[:, 0:4]), reads=ukeys, writes=[("UKS", c)])
                                P.op("pool", lambda e, c=c: e.tensor_copy(out=UKS[:, c, 4:20], in_=U[:, NT - 16:NT]), reads=ukeys, writes=[("UKS", c)])
                                bp = next_bank()
                                P.op("pe", lambda e, bp=bp, cc=cc, g=g: e.matmul(PS[bp][:, 0:4], lhsT=STT[0:60, cc * 128:(cc + 1) * 128], rhs=PW[0:60, g, :],
                                                                               start=True, stop=True),
                                     reads=["STT", "PW"], writes=[("PS", bp)])
                                P.op("dve", lambda e: e.tensor_tensor(out=TA[:, 5:NT], in0=U[:, 5:NT], in1=U[:, 4:NT - 1], op=ALU.add), reads=ukeys, writes=["TA"])
                                Sb, Sk = TA, "TA"
                                if w >= 4:
                                    P.op("dve", lambda e: e.tensor_tensor(out=TB[:, 7:NT], in0=TA[:, 7:NT], in1=TA[:, 5:NT - 2], op=ALU.add), reads=["TA"], writes=["TB"])
                                    Sb, Sk = TB, "TB"
                                if w >= 8:
                                    P.op("dve", lambda e: e.tensor_tensor(out=TA[:, 11:NT], in0=TB[:, 11:NT], in1=TB[:, 7:NT - 4], op=ALU.add), reads=["TB"], writes=["TA"])
                                    Sb, Sk = TA, "TA"
                                if w >= 16:
                                    P.op("dve", lambda e: e.tensor_tensor(out=TB[:, 19:NT], in0=TA[:, 19:NT], in1=TA[:, 11:NT - 8], op=ALU.add), reads=["TA"], writes=["TB"])
                                    Sb, Sk = TB, "TB"
                                dk = keys("DT", [cc], 0) + keys("DT", [cc], 1) + keys("DT", [cc], 2)
                                P.op("dve", lambda e, Sb=Sb, cc=cc, w=w: e.scalar_tensor_tensor(out=DT[:, cc, 19:NT], in0=Sb[:, 19:NT], scalar=1.0 / w,
                                                                                         in1=U[:, 19:NT], op0=ALU.mult, op1=ALU.subtract),
                                     reads=[Sk] + ukeys, writes=dk)
                                P.op("dve", lambda e, Sb=Sb, g=g: e.tensor_tensor(out=FX[:, 0:15], in0=Sb[:, M0:M0 + 15], in1=RFIX[:, g, :], op=ALU.mult),
                                     reads=[Sk, "RFIX"], writes=["FX"])
                                P.op("dve", lambda e, cc=cc: e.tensor_tensor(out=DT[:, cc, M0:M0 + 15], in0=FX[:, 0:15], in1=U[:, M0:M0 + 15], op=ALU.subtract),
                                     reads=["FX"] + ukeys, writes=dk)
                                P.op("dve", lambda e, cc=cc, w=w, bp=bp: e.scalar_tensor_tensor(out=DT[:, cc, 0:4], in0=U[:, 0:4], scalar=(1.0 / w - 1.0),
                                                                                         in1=PS[bp][:, 0:4], op0=ALU.mult, op1=ALU.add),
                                     reads=[("PS", bp)] + ukeys, writes=dk)
                        wv, wkey = wnext()
                        for ee in range(4):
                            c = 4 * g + ee

                            def evac(bank, c0, c1, b, c=c):
                                n = c1 - c0
                                P.op("dve", lambda e: e.scalar_tensor_tensor(out=HT[:, c, c0:c1], in0=PS[bank][:, 0:n], scalar=SCA[:, l, c:c + 1],
                                                                             in1=HT[:, c, c0:c1], op0=ALU.mult, op1=ALU.mult),
                                     reads=[("PS", bank), "SCA"], writes=[("HT", c, b)])
                            modeF_chunk(wv, wkey, ee, 4, lambda k, c0, c1: DT[:, k, c0:c1], lambda b: keys("DT", range(4), b), blocks, evac)
                    if ckpt('grp%d' % l):
                        return
                    for hf in range(2):
                        for q2 in range(2):
                            q4 = hf * 2 + q2
                            bank = next_bank()

                            def tr(e, q4=q4, bank=bank):
                                ins = None
                                for cc in range(4):
                                    c = q4 * 4 + cc
                                    ins = e.transpose(out=PS[bank][0:20, cc * 128:(cc + 1) * 128], in_=UKS[:, c, :], identity=IDF[:])
                                return ins
                            P.op("pe", tr, reads=[("UKS", q4 * 4 + i) for i in range(4)] + ["IDF"], writes=[("PS", bank)])
                            P.op("act", lambda e, q2=q2, bank=bank: e.copy(out=PST[0:20, q2 * 512:(q2 + 1) * 512], in_=PS[bank][0:20, :]),
                                 reads=[("PS", bank)], writes=[("PST", q2)])
                        pk = [("PST", 0), ("PST", 1)]
                        o_toks.append(P.op("sp", lambda e, hf=hf: e.dma_start(out=o_npp[l, :, hf * 1024:(hf + 1) * 1024], in_=PST[5:20, :]), reads=pk, dma=True))
                        for s in range(4):
                            o_toks.append(P.op("sp", lambda e, s=s, hf=hf: e.dma_start(out=o_nps[l, s, 14:15, hf * 1024:(hf + 1) * 1024], in_=PST[s:s + 1, :]), reads=pk, dma=True))
                    for s in range(4):
                        o_toks.append(P.op("sp", lambda e, s=s: e.dma_start(out=o_nps[l, s, 0:14, :], in_=stp[l, 15 * s + 1:15 * s + 15, :]), dma=True))
                    if ckpt('np%d' % l):
                        return
                    wout_phase(l, blocks)
                    dump(l)
                    if ckpt('L%d' % l):
                        return
                P.barrier()

            with ExitStack() as sbk:
                KTP = [sb("KTP%d" % i, [128, 4, 9 * 128], BF16, sbk) for i in range(2)]
                P.op("pool", lambda e: e.memset(KTP[0][:], 0.0), writes=[("KT", i) for i in range(9)])
                P.op("pool", lambda e: e.memset(KTP[1][:], 0.0), writes=[("KT", i) for i in range(9)])
                VB = sb("VB", [128, 9, 256], BF16, sbk)
                KTS = sb("KTS", [128, 4, 4, 128], BF16, sbk)
                VSB = sb("VSB", [128, 4, 256], BF16, sbk)
                RTS = [sb("RT%d" % i, [128, 4, 4, 8], F32, sbk) for i in range(2)]
                ktiles = [(H0, H0 + 128)] + [(M0 + 128 * i, M0 + 128 * (i + 1)) for i in range(NMT)]
                tiles_kv = [(S0, S0 + 4, 0, None)] + [(a, b_, i + 1, i) for i, (a, b_) in enumerate(ktiles)]

                def rope(src3, dst3, n, csidx, nh, sk, dk, rs=0):
                    RT = RTS[rs]
                    r0k, r1k, r2k, r3k = ("RT0", rs), ("RT1", rs), ("RT2", rs), ("RT3", rs)
                    cosb = bcast(CS[0:n, csidx, 0:8], [n, nh, 8], 1)
                    sinb = bcast(CS[0:n, csidx, 8:16], [n, nh, 8], 1)
                    rd = ["RT"]
                    P.op("act", lambda e: e.copy(out=dst3[:, :, 16:64], in_=src3[:, :, 16:64]), reads=[sk], writes=[dk])
                    P.op("dve", lambda e: e.tensor_tensor(out=RT[0:n, 0, 0:nh, :], in0=src3[:, :, 0:8], in1=cosb, op=ALU.mult), reads=[sk, "CS"], writes=[r0k])
                    P.op("dve", lambda e: e.tensor_tensor(out=RT[0:n, 1, 0:nh, :], in0=src3[:, :, 8:16], in1=sinb, op=ALU.mult), reads=[sk, "CS"], writes=[r1k])
                    P.op("dve", lambda e: e.tensor_tensor(out=RT[0:n, 2, 0:nh, :], in0=src3[:, :, 8:16], in1=cosb, op=ALU.mult), reads=[sk, "CS"], writes=[r2k])
                    P.op("dve", lambda e: e.tensor_tensor(out=RT[0:n, 3, 0:nh, :], in0=src3[:, :, 0:8], in1=sinb, op=ALU.mult), reads=[sk, "CS"], writes=[r3k])
                    P.op("dve", lambda e: e.tensor_tensor(out=dst3[:, :, 0:8], in0=RT[0:n, 0, 0:nh, :], in1=RT[0:n, 1, 0:nh, :], op=ALU.subtract),
                         reads=[r0k, r1k], writes=[dk])
                    P.op("dve", lambda e: e.tensor_tensor(out=dst3[:, :, 8:16], in0=RT[0:n, 2, 0:nh, :], in1=RT[0:n, 3, 0:nh, :], op=ALU.add),
                         reads=[r2k, r3k], writes=[dk])

                with ExitStack() as skv:
                    HTF = HT[:, 0:4, :].rearrange("p c t -> p (c t)").bitcast(F32)
                    KS = HTF[:, 0:1024].rearrange("p (s f) -> p s f", s=4)
                    VS = HTF[:, 1024:2048].rearrange("p (s f) -> p s f", s=4)
                    KRS = [sb("KR%d" % i, [128, 256], F32, skv) for i in range(2)]
                    KBDS = [sb("KBD%d" % i, [128, 4, 128], BF16, skv) for i in range(2)]
                    VR = sb("VR", [128, 256], F32, skv)
                    for s in range(4):
                        P.op("sp", lambda e, s=s: e.dma_start(out=KS[0:127, s, :], in_=ck[s, 1:128, :]), writes=[("KS", s)], dma=True)
                        P.op("sp", lambda e, s=s: e.dma_start(out=VS[0:127, s, :], in_=cv[s, 1:128, :]), writes=[("VS", s)], dma=True)
                    wv, wkey = wnext()
                    for (a, b_, csidx, kt) in tiles_kv:
                        n = b_ - a
                        bk = blk_of(a)
                        bank = next_bank()

                        def mmv(e, a=a, b_=b_, n=n, bank=bank):
                            ins = None
                            for k in range(NCH):
                                ins = e.matmul(PS[bank][0:n, 0:256], lhsT=XT[:, k, a:b_], rhs=wv[:, k, :], start=(k == 0), stop=(k == NCH - 1))
                            return ins
                        P.op("pe", mmv, reads=[wkey] + keys("XT", allch, bk), writes=[("PS", bank)])
                        if kt is None:
                            P.op("act", lambda e, bank=bank: e.copy(out=VR[0:4, :], in_=PS[bank][0:4, 0:256]), reads=[("PS", bank)], writes=["VR"])
                            for s in range(4):
                                P.op("sp", lambda e, s=s: e.dma_start(out=VS[127:128, s, :], in_=VR[s:s + 1, :]), reads=["VR"], writes=[("VS", s)], dma=True)
                        else:
                            P.op("act", lambda e, bank=bank, kt=kt: e.copy(out=VB[:, kt, :], in_=PS[bank][:, 0:256]), reads=[("PS", bank)], writes=[("VB", kt)])
                            if kt == 8:
                                P.op("dve", lambda e, bank=bank: e.tensor_copy(out=VR[:, :], in_=PS[bank][:, 0:256]), reads=[("PS", bank)], writes=["VR"])
                                o_toks.append(P.op("sp", lambda e: e.dma_start(out=o_nvp[:, :], in_=VR[:, :]), reads=["VR"], dma=True))
                    for s in range(4):
                        o_toks.append(P.op("sp", lambda e, s=s: e.dma_start(out=o_nvs[s, :, :], in_=VS[:, s, :]), reads=[("VS", s)], dma=True))
                        P.op("pool", lambda e, s=s: e.tensor_copy(out=VSB[:, s, :], in_=VS[:, s, :]), reads=[("VS", s)], writes=[("VSB", s)])
                    wv, wkey = wnext()

                    def kmm(tix):
                        (a, b_, csidx, kt) = tiles_kv[tix]
                        KR = KRS[tix % 2]
                        krk = ("KR", tix % 2)
                        n = b_ - a
                        bk = blk_of(a)
                        bank = next_bank()

                        def mmk(e):
                            ins = None
                            for k in range(NCH):
                                ins = e.matmul(PS[bank][0:n, 0:256], lhsT=XT[:, k, a:b_], rhs=wv[:, k, :], start=(k == 0), stop=(k == NCH - 1))
                            return ins
                        P.op("pe", mmk, reads=[wkey] + keys("XT", allch, bk), writes=[("PS", bank)])
                        src3 = PS[bank][0:n, 0:256].rearrange("p (h d) -> p h d", h=4)
                        dst3 = KR[0:n, :].rearrange("p (h d) -> p h d", h=4)
                        rope(src3, dst3, n, csidx, 4, ("PS", bank), krk, tix % 2)
                        if kt is None:
                            for s in range(4):
                                P.op("sp", lambda e, s=s: e.dma_start(out=KS[127:128, s, :], in_=KR[s:s + 1, :]), reads=[krk], writes=[("KS", s)], dma=True)
                        else:
                            if kt == 8:
                                o_toks.append(P.op("sp", lambda e: e.dma_start(out=o_nkp[:, :], in_=KR[:, :]), reads=[krk], dma=True))
                            KBD = KBDS[tix % 2]
                            KBD4 = KBD[:].rearrange("p g (a d) -> p g a d", a=2)
                            P.op("pool", lambda e: e.tensor_copy(out=KBD4, in_=bcast(dst3, [128, 4, 2, 64], 2)), reads=[krk], writes=[("KBD", tix % 2)])

                    def ktr(tix):
                        (a, b_, csidx, kt) = tiles_kv[tix]
                        if kt is None:
                            return
                        KBD = KBDS[tix % 2]
                        bt = next_bank()
                        pbt = PS[bt][:].bitcast(BF16)

                        def trk(e):
                            ins = None
                            for g in range(4):
                                ins = e.transpose(out=pbt[:, g * 128:(g + 1) * 128], in_=KBD[:, g, :], identity=IDB[:])
                            return ins
                        P.op("pe", trk, reads=[("KBD", tix % 2), "IDB"], writes=[("PS", bt)])
                        pv4 = pbt[:, 0:512].rearrange("p (g t) -> p g t", g=4)
                        P.op("act", lambda e: e.copy(out=KTP[0][0:64, :, kt * 128:(kt + 1) * 128], in_=pv4[0:64]), reads=[("PS", bt)], writes=[("KT", kt)])
                        P.op("act", lambda e: e.copy(out=KTP[1][64:128, :, kt * 128:(kt + 1) * 128], in_=pv4[64:128]), reads=[("PS", bt)], writes=[("KT", kt)])
                    kmm(0)
                    for tix in range(len(tiles_kv)):
                        if tix + 1 < len(tiles_kv):
                            kmm(tix + 1)
                        ktr(tix)
                    for s in range(4):
                        KBD = KBDS[s % 2]
                        KBD4 = KBD[:].rearrange("p g (a d) -> p g a d", a=2)
                        kbk = ("KBD", s % 2)
                        o_toks.append(P.op("sp", lambda e, s=s: e.dma_start(out=o_nks[s, :, :], in_=KS[:, s, :]), reads=[("KS", s)], dma=True))
                        src4 = KS[:, s, :].rearrange("p (h d) -> p h d", h=4)
                        P.op("pool", lambda e, src4=src4: e.tensor_copy(out=KBD4, in_=bcast(src4, [128, 4, 2, 64], 2)), reads=[("KS", s)], writes=[kbk])
                        bt = next_bank()
                        pbt = PS[bt][:].bitcast(BF16)

                        def trks(e, pbt=pbt):
                            ins = None
                            for g in range(4):
                                ins = e.transpose(out=pbt[:, g * 128:(g + 1) * 128], in_=KBD[:, g, :], identity=IDB[:])
                            return ins
                        P.op("pe", trks, reads=[kbk, "IDB"], writes=[("PS", bt)])
                        P.op("act", lambda e, pbt=pbt, s=s: e.copy(out=KTS[:, s, :, :], in_=pbt[:, 0:512].rearrange("p (g t) -> p g t", g=4)),
                             reads=[("PS", bt)], writes=[("KTS", s)])
                    P.barrier()

                if ckpt('kv'):

                    return
                with ExitStack() as sbl:
                    QT2 = [sb("QT2_%d" % i, [128, 4, 128], BF16, sbl) for i in range(2)]
                    QB = [sb("QB%d" % i, [128, 256], BF16, sbl) for i in range(2)]
                    ET = [HT[:, 4 * i:4 * i + 4, H0:H0 + 128] for i in range(4)] + [XT[:, 4 * i:4 * i + 4, H0:H0 + 128] for i in range(4)]
                    T1 = X32T[:, 0:4, H0:H0 + 128]
                    ESK = sb("ESK", [128, 32], F32, sbl)
                    ESK2 = sb("ESK2", [128, 4, 4], F32, sbl)
                    qtiles = [(S0, S0 + 4, 0)] + [(M0 + 128 * i, M0 + 128 * (i + 1), i + 2) for i in range(NMT)]
                    for j in range(2):
                        l = 2 + j
                        blocks = BLK23
                        P.op("sp", lambda e: e.dma_start(out=ESK[:], in_=sinks_b[j, :].partition_broadcast(128)), writes=["ESK"], dma=True)
                        P.op("act", lambda e: e.activation(out=ESK[:], in_=ESK[:], func=AF.Exp), writes=["ESK"])
                        ev = ESK[:].rearrange("p (g i two) -> p g i two", g=4, two=2)
                        P.op("dve", lambda e: e.tensor_copy(out=ESK2[0:64, :, :], in_=ev[0:64, :, :, 0]), reads=["ESK"], writes=["ESK2"])
                        P.op("dve", lambda e: e.tensor_copy(out=ESK2[64:128, :, :], in_=ev[64:128, :, :, 1]), reads=["ESK"], writes=["ESK2"])
                        gate_phase(blocks, True)
                        if ckpt('bgate%d' % l):
                            return
                        NTL = len(qtiles)
                        wqs = {}
                        qtr_done = set()

                        def qmm(gt):
                            g, ti = divmod(gt, NTL)
                            if g not in wqs:
                                wqs[g] = [wnext(), wnext()]
                            wq = wqs[g]
                            (a, b_, csidx) = qtiles[ti]
                            n = b_ - a
                            bk = blk_of(a)
                            for i in range(2):
                                wv, wkey = wq[i]
                                bank = next_bank()

                                def mmq(e, bank=bank, wv=wv):
                                    ins = None
                                    for k in range(NCH):
                                        ins = e.matmul(PS[bank][0:n, 0:256], lhsT=XT[:, k, a:b_], rhs=wv[:, k, :], start=(k == 0), stop=(k == NCH - 1))
                                    return ins
                                P.op("pe", mmq, reads=[wkey] + keys("XT", allch, bk), writes=[("PS", bank)])
                                qb = QB[i]
                                qbk = ("QB", i)
                                src3 = PS[bank][0:n, 0:256].rearrange("p (h d) -> p h d", h=4)
                                dst3 = qb[0:n, :].rearrange("p (h d) -> p h d", h=4)
                                rope(src3, dst3, n, csidx, 4, ("PS", bank), qbk, i)

                        def qtr(gt):
                            g, ti = divmod(gt, NTL)
                            (a, b_, csidx) = qtiles[ti]
                            n = b_ - a
                            qt = QT2[gt % 2]
                            for i in range(2):
                                qb = QB[i]
                                qbk = ("QB", i)
                                bt = next_bank()
                                pbt = PS[bt][:].bitcast(BF16)

                                def trq(e, pbt=pbt, qb=qb):
                                    ins = None
                                    for cc in range(2):
                                        ins = e.transpose(out=pbt[:, cc * 128:cc * 128 + n], in_=qb[0:n, 128 * cc:128 * (cc + 1)], identity=IDB[0:n, 0:n])
                                    return ins
                                P.op("pe", trq, reads=[qbk, "IDB"], writes=[("PS", bt)])
                                P.op("act", lambda e, pbt=pbt, i=i: e.copy(out=qt[:, 2 * i:2 * i + 2, 0:n],
                                                                          in_=pbt[:, 0:256].rearrange("p (c t) -> p c t", c=2)[:, :, 0:n]),
                                     reads=[("PS", bt)], writes=[("QT2", gt % 2, i)])
                            qtr_done.add(gt)

                        units = []
                        for g in range(4):
                            units.append((g * NTL, 0, 4, [(KTS[:, s_, g, :], VSB[:, s_, 64 * g:64 * (g + 1)], "S", ("KTS", s_), ("VSB", s_)) for s_ in range(4)], S0, 0, g))
                            for qb_ in range(NMT):
                                kl = []
                                for kt, mi in ((qb_, 2 if qb_ == 0 else 0), (qb_ + 1, 1)):
                                    kl.append(((KTP[0][:, g, kt * 128:(kt + 1) * 128], KTP[1][:, g, kt * 128:(kt + 1) * 128]), VB[:, kt, 64 * g:64 * (g + 1)], mi, ("KT", kt), ("VB", kt)))
                                units.append((g * NTL + qb_ + 1, 0, 128, kl, M0 + 128 * qb_, 1 if qb_ < 4 else 2, g))

                        def stageA(ui):
                            (gt, off, nq, kl, ha, hb, g) = units[ui]
                            assert gt in qtr_done, (gt, ui)
                            N = 4 * nq
                            qt = QT2[gt % 2]
                            qkeys = [("QT2", gt % 2, 0), ("QT2", gt % 2, 1)]
                            banks = []
                            ets = []
                            if kl[0][2] == "S":
                                bks = [next_bank(), next_bank()]

                                def mms_s(e):
                                    ins = None
                                    for s_, (kTap, vap, mi, kkey, vkey) in enumerate(kl):
                                        for par in range(2):
                                            p0, p1 = 64 * par, 64 * par + 64
                                            ins = e.matmul(PS[bks[par]][:, 0:16].rearrange("p (c s) -> p c s", s=4)[:, :, s_],
                                                           lhsT=kTap[p0:p1, :], rhs=qt[p0:p1, :, s_:s_ + 1], start=True, stop=True)
                                    return ins
                                P.op("pe", mms_s, reads=[x[3] for x in kl] + qkeys, writes=[("PS", bks[0]), ("PS", bks[1])])
                                for par in range(2):
                                    eti = (ui % 2) * 4 + par
                                    et = ET[eti]
                                    ek = ("ET", eti)
                                    P.op("act", lambda e, et=et, par=par: e.activation(out=et[:, :, 0:4], in_=PS[bks[par]][:, 0:16].rearrange("p (c t) -> p c t", c=4),
                                                                                   func=AF.Exp, scale=0.125),
                                         reads=[("PS", bks[par])], writes=[ek])
                                    ets.append((et, ek, None, None, par))
                                return ets
                            for ki, (kTap, vap, mi, kkey, vkey) in enumerate(kl):
                                for par in range(2):
                                    banks.append((next_bank(), ki, par, kTap, mi))

                            def mms(e):
                                ins = None
                                for (bank, ki, par, kTap, mi) in banks:
                                    p0, p1 = 64 * par, 64 * par + 64
                                    if isinstance(kTap, tuple):
                                        ins = e.matmul(PS[bank][:, 0:N], lhsT=kTap[par], rhs=qt[:, :, off:off + nq], start=True, stop=(mi is None))
                                    else:
                                        ins = e.matmul(PS[bank][:, 0:N], lhsT=kTap[p0:p1, :], rhs=qt[p0:p1, :, off:off + nq], start=True, stop=(mi is None))
                                for (bank, ki, par, kTap, mi) in banks:
                                    if mi is not None:
                                        ins = e.matmul(PS[bank][:, 0:N], lhsT=IDB[:], rhs=MASK[:, mi, 0:N], start=False, stop=True)
                                return ins
                            P.op("pe", mms, reads=[x[3] for x in kl] + ["MASK", "IDB"] + qkeys, writes=[("PS", bk_[0]) for bk_ in banks])
                            for (bank, ki, par, kTap, mi) in banks:
                                eti = (ui % 2) * 4 + ki * 2 + par
                                et = ET[eti]
                                ek = ("ET", eti)
                                P.op("act", lambda e, et=et, bank=bank: e.activation(out=et[:, :, 0:nq], in_=PS[bank][:, 0:N].rearrange("p (c t) -> p c t", c=4),
                                                                                 func=AF.Exp, scale=0.125),
                                     reads=[("PS", bank)], writes=[ek])
                                ets.append((et, ek, kl[ki][1], kl[ki][4], par))
                            return ets

                        def stageB(ui, ets):
                            (gt, off, nq, kl, ha, hb, g) = units[ui]
                            N = 4 * nq
                            bo = next_bank()
                            bd = next_bank()

                            def mmo_s(e):
                                ins = None
                                for par in range(2):
                                    p0, p1 = 64 * par, 64 * par + 64
                                    et = [x for x in ets if x[4] == par][0][0]
                                    for s_, (kTap, vap, mi, kkey, vkey) in enumerate(kl):
                                        ins = e.matmul(PS[bo][p0:p1, 0:16].rearrange("p (c s) -> p c s", s=4)[:, :, s_], lhsT=vap, rhs=et[:, :, s_:s_ + 1], start=True, stop=True)
                                    for s_, (kTap, vap, mi, kkey, vkey) in enumerate(kl):
                                        ins = e.matmul(PS[bd][p0:p1, 0:16].rearrange("p (c s) -> p c s", s=4)[:, :, s_], lhsT=ONES64[:], rhs=et[:, :, s_:s_ + 1], start=True, stop=True)
                                return ins

                            def mmo(e):
                                ins = None
                                for par in range(2):
                                    p0, p1 = 64 * par, 64 * par + 64
                                    sel = [x for x in ets if x[4] == par]
                                    for ii, (et, ek, vap, vkey, _) in enumerate(sel):
                                        ins = e.matmul(PS[bo][p0:p1, 0:N], lhsT=vap, rhs=et[:, :, 0:nq], start=(ii == 0), stop=(ii == len(sel) - 1))
                                    for ii, (et, ek, vap, vkey, _) in enumerate(sel):
                                        ins = e.matmul(PS[bd][p0:p1, 0:N], lhsT=ONES64[:], rhs=et[:, :, 0:nq], start=(ii == 0), stop=(ii == len(sel) - 1))
                                return ins
                            if kl[0][2] == "S":
                                P.op("pe", mmo_s, reads=[x[1] for x in ets] + [x[4] for x in kl] + ["ONES64"], writes=[("PS", bo), ("PS", bd)])
                            else:
                                P.op("pe", mmo, reads=[x[1] for x in ets] + [x[3] for x in ets] + ["ONES64"], writes=[("PS", bo), ("PS", bd)])
                            R = RR[ui % 2]
                            rk = ("RR", ui % 2)
                            P.op("dve", lambda e: e.tensor_tensor(out=R[:, 0:N].rearrange("p (c t) -> p c t", c=4),
                                                                  in0=PS[bd][:, 0:N].rearrange("p (c t) -> p c t", c=4),
                                                                  in1=bcast(ESK2[:, g, :], [128, 4, nq], 2), op=ALU.add),
                                 reads=[("PS", bd), "ESK2"], writes=[rk])
                            P.op("act", lambda e: e.activation(out=R[:, 0:N], in_=R[:, 0:N], func=AF.Ln), writes=[rk])
                            P.op("act", lambda e: e.activation(out=R[:, 0:N], in_=R[:, 0:N], func=AF.Exp, scale=-1.0), writes=[rk])
                            P.op("dve", lambda e: e.tensor_tensor(out=T1[:, :, 0:nq], in0=PS[bo][:, 0:N].rearrange("p (c t) -> p c t", c=4),
                                                                  in1=R[:, 0:N].rearrange("p (c t) -> p c t", c=4), op=ALU.mult),
                                 reads=[("PS", bo), rk], writes=["T1"])
                            hk = keys("HT", range(4 * g, 4 * g + 4), hb)
                            P.op("dve", lambda e: e.tensor_tensor(out=HT[:, 4 * g:4 * g + 4, ha:ha + nq],
                                                                  in0=T1[:, :, 0:nq],
                                                                  in1=HT[:, 4 * g:4 * g + 4, ha:ha + nq], op=ALU.mult),
                                 reads=["T1"], writes=hk)

                        TT = 4 * NTL
                        last_unit = {}
                        for ui_, u_ in enumerate(units):
                            last_unit[u_[0]] = ui_
                        qmm(0)
                        qtr(0)
                        qmm(1)
                        qtr(1)
                        pend = stageA(0)
                        qmm(2)
                        pend_tr = 2
                        nextq = 3
                        for ui in range(len(units)):
                            if ui + 1 < len(units) and units[ui + 1][0] not in qtr_done:
                                assert pend_tr == units[ui + 1][0], (pend_tr, units[ui + 1][0])
                                qtr(pend_tr)
                                pend_tr = None
                            nxt = stageA(ui + 1) if ui + 1 < len(units) else None
                            if pend_tr is not None and last_unit[pend_tr - 2] <= ui + 1:
                                qtr(pend_tr)
                                pend_tr = None
                            stageB(ui, pend)
                            if pend_tr is None and nextq < TT:
                                qmm(nextq)
                                pend_tr = nextq
                                nextq += 1
                            pend = nxt
                        assert nextq == TT and pend_tr is None, (nextq, pend_tr)
                        if ckpt('attn%d' % l):
                            return
                        if l == 3:
                            HTO = HT[:, 0:8, :].rearrange("p c t -> p (c t)").bitcast(F32)
                            OST = [HTO[:, 0:2048], HTO[:, 2048:4096]]
                            htk = keys("HT", range(8), 0) + keys("HT", range(8), 1) + keys("HT", range(8), 2)
                            out_units = [(S0, S0 + 4, None)] + [(M0 + 128 * i, M0 + 128 * (i + 1), i) for i in range(NMT)]
                            octr = [0]
                            obank = [0]

                            def emit_out(a, b_, mt):
                                n = b_ - a
                                oi = octr[0] % 2
                                octr[0] += 1
                                ost = OST[oi]
                                okey = ("OST", oi)
                                bk = blk_of(a)
                                for cg in range(4):
                                    bank = obank[0] % 5
                                    obank[0] += 1

                                    def tro(e, cg=cg, bank=bank):
                                        ins = None
                                        for cc in range(4):
                                            c = cg * 4 + cc
                                            ins = e.transpose(out=PS[bank][0:n, cc * 128:(cc + 1) * 128], in_=X32T[:, c, a:b_], identity=IDF[:])
                                        return ins
                                    P.op("pe", tro, reads=["IDF"] + keys("X32", range(cg * 4, cg * 4 + 4), bk), writes=[("PS", bank)])
                                    if cg % 2 == 0:
                                        P.op("dve", lambda e, cg=cg, bank=bank: e.tensor_copy(out=ost[0:n, cg * 512:(cg + 1) * 512], in_=PS[bank][0:n, :]),
                                             reads=[("PS", bank)], writes=[okey + (cg,)] + htk)
                                    else:
                                        P.op("act", lambda e, cg=cg, bank=bank: e.copy(out=ost[0:n, cg * 512:(cg + 1) * 512], in_=PS[bank][0:n, :]),
                                             reads=[("PS", bank)], writes=[okey + (cg,)] + htk)
                                rk = [okey + (cg,) for cg in range(4)]
                                if mt is None:
                                    o_toks.append(P.op("sp", lambda e: e.dma_start(out=o_ys[:, :], in_=ost[0:4, :]), reads=rk, dma=True))
                                else:
                                    o_toks.append(P.op("sp", lambda e: e.dma_start(out=o_yp[128 * mt:128 * (mt + 1), :], in_=ost[:, :]), reads=rk, dma=True))

                            def between3(bi):
                                sel = [out_units[0]] if bi == 0 else (out_units[1:5] if bi == 1 else out_units[5:9])
                                for (a, b_, mt) in sel:
                                    emit_out(a, b_, mt)
                            wout_phase(l, blocks, between3)
                        else:
                            wout_phase(l, blocks)
                        dump(l)
                        if ckpt('L%d' % l):
                            return
                    P.barrier()

        if stage != 'const':
            phases()
        if stage is not None:
            P.barrier()
            if debug:
                P.op('sp', lambda e: e.dma_start(out=o_dbg[3, :, :, :], in_=X32T[:]), dma=True)
        P.final_wait("sp")
    return nc


_NC_CACHE = {}


def host_consts(core):
    b, h = core // 2, core % 2
    start = 1024 * h
    inv = (np.float32(500000.0) ** (-np.arange(0, 16, 2, dtype=np.float32) / np.float32(16.0))).astype(np.float32)
    pos = np.zeros((128, 10), np.float32)
    pos[:, 0] = PAST
    for t in range(9):
        pos[:, t + 1] = start - 128 + 128 * t + np.arange(128)
    ang = (pos[:, :, None].astype(np.float32) * inv[None, None, :]).astype(np.float32)
    cs = np.concatenate([np.cos(ang), np.sin(ang)], axis=-1).astype(np.float32)
    k = np.arange(128)[:, None]
    q = np.arange(128)[None, :]
    mprev = (k > q).astype(np.float32)
    mown = (k <= q).astype(np.float32)
    mask = np.stack([mprev, mown, mprev * float(h)], axis=1)
    mask = np.tile((mask - 1.0) * 30000.0, (1, 1, 4)).astype(ml_dtypes.bfloat16)
    hm = np.full((128, 1), float(h), np.float32)
    pw = np.zeros((60, 4, 4), np.float32)
    for g, w in enumerate(WINS):
        for s in range(4):
            for jj in range(15 - (w - 1), 15):
                pw[15 * s + jj, g, s] = 1.0 / w
    rfix = np.zeros((128, 4, 15), np.float32)
    for g, w in enumerate(WINS):
        for t in range(15):
            cnt = min(w, t + 1) if h == 0 else w
            rfix[:, g, t] = 1.0 / cnt
    return cs, mask, hm, pw, rfix


def kernel(x_prompt, x_sample, state_pool, cache_k, cache_v, w_in_a, w_grp_a, scale_a, w_out_a,
           w_kv, w_in_b, sinks_b, w_out_b, ln_g, ln_b, _debug=False, _stage=None, _cores=None):
    f = lambda a: np.ascontiguousarray(np.asarray(a, dtype=np.float32))
    x_prompt, x_sample, state_pool, cache_k, cache_v = map(f, (x_prompt, x_sample, state_pool, cache_k, cache_v))
    shared = {"w_in_a": f(w_in_a), "w_grp_a": f(w_grp_a), "scale_a": f(scale_a), "w_out_a": f(w_out_a), "w_kv": f(w_kv),
              "w_in_b": f(w_in_b), "sinks_b": f(sinks_b), "w_out_b": f(w_out_b), "ln_g": f(ln_g), "ln_b": f(ln_b)}
    key = (bool(_debug), _stage)
    if key not in _NC_CACHE:
        _NC_CACHE[key] = build(debug=bool(_debug), stage=_stage)
    nc = _NC_CACHE[key]
    in_maps = []
    for core in range(8):
        b, h = core // 2, core % 2
        start = 1024 * h
        xin = np.zeros((NT, D), np.float32)
        xin[0:4] = x_sample[4 * core:4 * core + 4, 0]
        if h == 1:
            xin[E0:M0] = x_prompt[b, start - 160:start]
        xin[M0:] = x_prompt[b, start:start + 1024]
        cs, mask, hm, pw, rfix = host_consts(core)
        m = dict(shared)
        m.update({"xin": xin,
                  "stp": np.ascontiguousarray(state_pool[:, 4 * core:4 * core + 4].reshape(2, 60, D)),
                  "ck": np.ascontiguousarray(cache_k[4 * core:4 * core + 4].reshape(4, 128, 256)),
                  "cv": np.ascontiguousarray(cache_v[4 * core:4 * core + 4].reshape(4, 128, 256)),
                  "c_cs": cs, "c_mask": mask, "c_hm": hm, "c_pw": pw, "c_rfix": rfix})
        in_maps.append(m)
    if _cores is not None:
        res = run_bass_kernel_spmd(nc, [in_maps[c] for c in _cores], core_ids=list(range(len(_cores))))
        return res.results
    res = run_bass_kernel_spmd(nc, in_maps, core_ids=list(range(8)))
    R = res.results
    y_prompt = np.zeros((4, 2048, D), np.float32)
    y_sample = np.zeros((32, 1, D), np.float32)
    npp = np.zeros((2, 4, 15, D), np.float32)
    nps = np.zeros((2, 32, 15, D), np.float32)
    nkp = np.zeros((4, 128, 4, 64), np.float32)
    nvp = np.zeros((4, 128, 4, 64), np.float32)
    nks = np.zeros((32, 128, 4, 64), np.float32)
    nvs = np.zeros((32, 128, 4, 64), np.float32)
    for core in range(8):
        b, h = core // 2, core % 2
        r = R[core]
        y_prompt[b, 1024 * h:1024 * (h + 1)] = r["o_yp"]
        y_sample[4 * core:4 * core + 4, 0] = r["o_ys"]
        nps[:, 4 * core:4 * core + 4] = r["o_nps"]
        nks[4 * core:4 * core + 4] = r["o_nks"].reshape(4, 128, 4, 64)
        nvs[4 * core:4 * core + 4] = r["o_nvs"].reshape(4, 128, 4, 64)
        if h == 1:
            npp[:, b] = r["o_npp"]
            nkp[b] = r["o_nkp"].reshape(128, 4, 64)
            nvp[b] = r["o_nvp"].reshape(128, 4, 64)
    if _debug:
        return (y_prompt, y_sample, npp, nps, nkp, nvp, nks, nvs), [r["o_dbg"] for r in R]
    return (y_prompt, y_sample, npp, nps, nkp, nvp, nks, nvs)
```

```python
import numpy as np
import ml_dtypes
import concourse.bass as bass
import concourse.mybir as mybir
from concourse.bass_utils import run_bass_kernel_spmd
from contextlib import ExitStack

F32 = mybir.dt.float32
BF16 = mybir.dt.bfloat16
ALU = mybir.AluOpType
AF = mybir.ActivationFunctionType

D = 2048
NCH = 16
NT = 1188
S0, E0, H0, M0 = 0, 4, 36, 164
NMT = 8
BLK01 = [(0, 164), (164, 676), (676, 1188)]
BLK23 = [(0, 4), (164, 676), (676, 1188)]
ALPHA = 8.0 ** 0.25
LN_EPS = 1e-5
WINS = (2, 4, 8, 16)
PAST = 16384
DEBUG = False


def blk_of(c0):
    return 0 if c0 < 164 else (1 if c0 < 676 else 2)


class Tok:
    __slots__ = ("sem", "val", "eng", "dma")

    def __init__(self, sem, val, eng, dma):
        self.sem, self.val, self.eng, self.dma = sem, val, eng, dma


class Prog:
    KQ = 12

    def __init__(self, nc, stack):
        self.nc = nc
        self.h = {"pe": nc.tensor, "act": nc.scalar, "dve": nc.vector, "pool": nc.gpsimd, "sp": nc.sync}
        self.sem = {e: stack.enter_context(nc.semaphore("s_" + e)) for e in self.h}
        self.cnt = {e: 0 for e in self.h}
        self.waited = {e: {} for e in self.h}
        self.lastw = {}
        self.readers = {}
        self.dq = {}
        for q in ("sp", "pool", "act"):
            self.dq[q] = {"sems": [stack.enter_context(nc.semaphore("d_%s%d" % (q, i))) for i in range(self.KQ)], "n": 0}
        self.semid = {}
        self.all_toks = []
        self.out_toks = []

    def _wait(self, e, sem, val):
        key = id(sem)
        self.semid[key] = sem
        if self.waited[e].get(key, 0) < val:
            self.h[e].wait_ge(sem, val)
            self.waited[e][key] = val

    def op(self, e, fn, reads=(), writes=(), dma=False, extra=()):
        deps = list(extra)
        psr = [k for k in reads if isinstance(k, tuple) and k[0] == "PS"]
        if psr:
            reads = [k for k in reads if k not in psr]
            writes = list(writes) + psr
        for k in reads:
            t = self.lastw.get(k)
            if t is not None:
                deps.append(t)
        for k in writes:
            t = self.lastw.get(k)
            if t is not None:
                deps.append(t)
            deps.extend(self.readers.get(k, ()))
        need = {}
        for t in deps:
            if e == "pe" and (not dma) and t.eng == "pe" and (not t.dma):
                continue
            key = id(t.sem)
            self.semid[key] = t.sem
            if need.get(key, 0) < t.val:
                need[key] = t.val
        for key, val in need.items():
            self._wait(e, self.semid[key], val)
        if dma:
            q = self.dq[e]
            j = q["n"]
            q["n"] += 1
            sem = q["sems"][j % self.KQ]
            if j >= self.KQ:
                self._wait(e, sem, 16 * (j // self.KQ))
            ins = fn(self.h[e])
            val = 16 * (j // self.KQ + 1)
            ins.then_inc(sem, 16)
            tok = Tok(sem, val, e, True)
        else:
            ins = fn(self.h[e])
            self.cnt[e] += 1
            ins.then_inc(self.sem[e], 1)
            tok = Tok(self.sem[e], self.cnt[e], e, False)
        for k in reads:
            self.readers.setdefault(k, []).append(tok)
        for k in writes:
            self.lastw[k] = tok
            self.readers[k] = []
        self.all_toks.append(tok)
        if len(self.all_toks) > 4096:
            self._compact()
        return tok

    def _compact(self):
        best = {}
        for t in self.all_toks:
            k = id(t.sem)
            if k not in best or best[k].val < t.val:
                best[k] = t
        self.all_toks = list(best.values())

    def barrier(self):
        self._compact()
        toks = list(self.all_toks)
        for e in self.h:
            for t in toks:
                self._wait(e, t.sem, t.val)
        self.lastw = {}
        self.readers = {}

    def final_wait(self, e="sp"):
        self._compact()
        for t in self.all_toks:
            self._wait(e, t.sem, t.val)


def bcast(ap, shape, axis):
    return ap.unsqueeze(axis).broadcast_to(shape)


class _Stop(Exception):
    pass


def build(debug=False, stage=None):
    nc = bass.Bass("TRN2", target_bir_lowering=False, dynamic_dma_scratch_size=4096)

    def din(name, shape, dt=F32):
        return nc.dram_tensor(name, list(shape), dt, kind="ExternalInput").ap()

    def dout(name, shape, dt=F32):
        return nc.dram_tensor(name, list(shape), dt, kind="ExternalOutput").ap()

    xin = din("xin", [NT, D])
    stp = din("stp", [2, 60, D])
    ck = din("ck", [4, 128, 256])
    cv = din("cv", [4, 128, 256])
    w_in_a = din("w_in_a", [2, D, 2 * D])
    w_grp_a = din("w_grp_a", [2, 4, 512, 512])
    scale_a = din("scale_a", [2, D])
    w_out_a = din("w_out_a", [2, D, D])
    w_kv = din("w_kv", [D, 512])
    w_in_b = din("w_in_b", [2, D, 2 * D])
    sinks_b = din("sinks_b", [2, 32])
    w_out_b = din("w_out_b", [2, D, D])
    ln_g = din("ln_g", [4, D])
    ln_b = din("ln_b", [4, D])
    c_cs = din("c_cs", [128, 10, 16])
    c_mask = din("c_mask", [128, 3, 512], BF16)
    c_hm = din("c_hm", [128, 1])
    c_pw = din("c_pw", [60, 4, 4])
    c_rfix = din("c_rfix", [128, 4, 15])

    o_yp = dout("o_yp", [1024, D])
    o_ys = dout("o_ys", [4, D])
    o_npp = dout("o_npp", [2, 15, D])
    o_nps = dout("o_nps", [2, 4, 15, D])
    o_nkp = dout("o_nkp", [128, 256])
    o_nvp = dout("o_nvp", [128, 256])
    o_nks = dout("o_nks", [4, 128, 256])
    o_nvs = dout("o_nvs", [4, 128, 256])
    o_dbg = dout("o_dbg", [4, 128, NCH, NT]) if debug else None

    with ExitStack() as st:
        P = Prog(nc, st)

        def sb(name, shape, dt=F32, stack=st):
            return stack.enter_context(nc.sbuf_tensor(name, list(shape), dt))

        X32T = sb("X32T", [128, NCH, NT])
        XT = sb("XT", [128, NCH, NT], BF16)
        HT = sb("HT", [128, NCH, NT], BF16)
        WB = [sb("WB%d" % i, [128, 4096], BF16) for i in range(3)]
        IDB = sb("IDB", [128, 128], BF16)
        IDF = sb("IDF", [128, 128])
        ONESM = sb("ONESM", [128, 128], BF16)
        ONES64 = sb("ONES64", [128, 64], BF16)
        EPS = sb("EPS", [128, 1])
        HM = sb("HM", [128, 1])
        GAM = sb("GAM", [128, 4, NCH])
        BET = sb("BET", [128, 4, NCH])
        SCA = sb("SCA", [128, 2, NCH])
        CS = sb("CS", [128, 10, 16])
        MASK = sb("MASK", [128, 3, 512], BF16)
        PW = sb("PW", [60, 4, 4])
        RFIX = sb("RFIX", [128, 4, 15])
        RR = [sb("RR%d" % i, [128, 512]) for i in range(2)]
        PS = [st.enter_context(nc.psum_tensor("ps%d" % i, [128, 512], F32)) for i in range(8)]
        bank_ctr = [0]

        def next_bank():
            b = bank_ctr[0] % 8
            bank_ctr[0] += 1
            return b

        P.op("sp", lambda e: e.dma_start(out=HM[:], in_=c_hm[:, :]), writes=["HM"], dma=True)
        P.op("sp", lambda e: e.dma_start(out=CS[:], in_=c_cs[:, :, :]), writes=["CS"], dma=True)
        P.op("sp", lambda e: e.dma_start(out=MASK[:], in_=c_mask[:, :, :]), writes=["MASK"], dma=True)
        P.op("sp", lambda e: e.dma_start(out=PW[:], in_=c_pw[:, :, :]), writes=["PW"], dma=True)
        P.op("sp", lambda e: e.dma_start(out=RFIX[:], in_=c_rfix[:, :, :]), writes=["RFIX"], dma=True)
        P.op("pool", lambda e: e.memset(IDF[:], 0.0), writes=["IDF"])
        P.op("pool", lambda e: e.iota(IDF[:], [[1, 128]], base=0, channel_multiplier=-1, allow_small_or_imprecise_dtypes=True), writes=["IDF"])
        P.op("dve", lambda e: e.tensor_scalar(out=IDF[:], in0=IDF[:], scalar1=0.0, scalar2=None, op0=ALU.is_equal), writes=["IDF"])
        P.op("dve", lambda e: e.tensor_copy(out=IDB[:], in_=IDF[:]), reads=["IDF"], writes=["IDB"])
        P.op("dve", lambda e: e.memset(ONESM[:], 1.0 / D), writes=["ONESM"])
        with ExitStack() as sprm:
            PRM = sb("PRM", [64, 3, 128], F32, sprm)
            P.op("sp", lambda e: e.dma_start(out=PRM[0:64, 0, :], in_=ln_g.rearrange("l (c p) -> (l c) p", p=128)), writes=["PRM0"], dma=True)
            P.op("sp", lambda e: e.dma_start(out=PRM[0:64, 1, :], in_=ln_b.rearrange("l (c p) -> (l c) p", p=128)), writes=["PRM1"], dma=True)
            P.op("sp", lambda e: e.dma_start(out=PRM[0:32, 2, :], in_=scale_a.rearrange("l (c p) -> (l c) p", p=128)), writes=["PRM2"], dma=True)

            def trp(e):
                e.transpose(out=PS[0][:, 0:64], in_=PRM[0:64, 0, :], identity=IDF[0:64, 0:64])
                e.transpose(out=PS[0][:, 64:128], in_=PRM[0:64, 1, :], identity=IDF[0:64, 0:64])
                return e.transpose(out=PS[0][:, 128:160], in_=PRM[0:32, 2, :], identity=IDF[0:32, 0:32])
            P.op("pe", trp, reads=["PRM0", "PRM1", "PRM2", "IDF"], writes=[("PS", 0)])
            P.op("dve", lambda e: e.tensor_copy(out=GAM[:].rearrange("p l c -> p (l c)"), in_=PS[0][:, 0:64]), reads=[("PS", 0)], writes=["GAM"])
            P.op("dve", lambda e: e.tensor_copy(out=BET[:].rearrange("p l c -> p (l c)"), in_=PS[0][:, 64:128]), reads=[("PS", 0)], writes=["BET"])
            P.op("dve", lambda e: e.tensor_copy(out=SCA[:].rearrange("p l c -> p (l c)"), in_=PS[0][:, 128:160]), reads=[("PS", 0)], writes=["SCA"])
            P.barrier()
        P.op("dve", lambda e: e.memset(ONES64[:], 1.0), writes=["ONES64"])
        P.op("dve", lambda e: e.memset(EPS[:], LN_EPS), writes=["EPS"])

        def wview(buf, nchunk, ncol):
            return WB[buf][:, 0:nchunk * ncol].rearrange("p (c f) -> p c f", c=nchunk)

        wstate = {"i": 0, "pending": None}
        wlist = []

        def wissue(idx):
            ap2d, nchunk, ncol = wlist[idx]
            buf = idx % 3
            dst = wview(buf, nchunk, ncol)
            src = ap2d.rearrange("(c p) f -> p c f", p=128)
            P.op("pool", lambda e: e.dma_start(out=dst, in_=src), writes=[("W", buf)], dma=True)

        wstate["issued"] = -1

        def wensure(upto):
            while wstate["issued"] < min(upto, len(wlist) - 1):
                wstate["issued"] += 1
                wissue(wstate["issued"])

        def wnext(prefetch=True):
            i = wstate["i"]
            wensure(i + 1 if prefetch else i)
            wstate["i"] = i + 1
            ap2d, nchunk, ncol = wlist[i]
            return wview(i % 3, nchunk, ncol), ("W", i % 3)

        def wprefetch():
            wensure(wstate["issued"] + 1)

        for l in range(2):
            for j in range(8):
                wlist.append((w_in_a[l, :, D + 256 * j:D + 256 * (j + 1)], 16, 256))
            for g in range(4):
                for i in range(2):
                    wlist.append((w_in_a[l, :, 512 * g + 256 * i:512 * g + 256 * (i + 1)], 16, 256))
                wlist.append((w_grp_a[l, g, :, :], 4, 512))
            for j in range(8):
                wlist.append((w_out_a[l, :, 256 * j:256 * (j + 1)], 16, 256))
        wlist.append((w_kv[:, 256:512], 16, 256))
        wlist.append((w_kv[:, 0:256], 16, 256))
        for l in range(2):
            for j in range(8):
                wlist.append((w_in_b[l, :, D + 256 * j:D + 256 * (j + 1)], 16, 256))
            for j in range(8):
                wlist.append((w_in_b[l, :, 256 * j:256 * (j + 1)], 16, 256))
            for j in range(8):
                wlist.append((w_out_b[l, :, 256 * j:256 * (j + 1)], 16, 256))

        allch = list(range(NCH))

        def keys(name, chs, b):
            return [(name, c, b) for c in chs]

        def modeF_chunk(wv, wkey, fsub, nK, rhs_of, rhs_keys_of, blocks, evac):
            for (c0, c1) in blocks:
                n = c1 - c0
                b = blk_of(c0)
                bank = next_bank()

                def mm(e, c0=c0, c1=c1, n=n, bank=bank):
                    ins = None
                    for k in range(nK):
                        ins = e.matmul(PS[bank][:, 0:n], lhsT=wv[:, k, fsub * 128:(fsub + 1) * 128], rhs=rhs_of(k, c0, c1),
                                       start=(k == 0), stop=(k == nK - 1))
                    return ins
                P.op("pe", mm, reads=[wkey] + rhs_keys_of(b), writes=[("PS", bank)])
                evac(bank, c0, c1, b)

        def layer_norm(l, blocks):
            nb = len(blocks)
            info = [(c0, c1, c1 - c0, blk_of(c0)) for (c0, c1) in blocks]
            bms = []
            for (c0, c1, n, b) in info:
                bm = next_bank()
                bms.append(bm)

                def mm_mu(e, bm=bm, c0=c0, c1=c1, n=n):
                    ins = None
                    for k in range(NCH):
                        ins = e.matmul(PS[bm][:, 0:n], lhsT=ONESM[:], rhs=XT[:, k, c0:c1], start=(k == 0), stop=(k == NCH - 1))
                    return ins
                P.op("pe", mm_mu, reads=keys("XT", allch, b) + ["ONESM"], writes=[("PS", bm)])
            for (c0, c1, n, b), bm in zip(info, bms):
                P.op("dve", lambda e: e.tensor_tensor(out=X32T[:, :, c0:c1], in0=X32T[:, :, c0:c1],
                                                      in1=bcast(PS[bm][:, 0:n], [128, NCH, n], 1), op=ALU.subtract),
                     reads=[("PS", bm)], writes=keys("X32", allch, b))
            CS_ = 9
            for (c0, c1, n, b) in info:
                P.op("act", lambda e: e.activation(out=XT[:, 0:CS_, c0:c1], in_=X32T[:, 0:CS_, c0:c1], func=AF.Square),
                     reads=keys("X32", range(CS_), b), writes=keys("XT", range(CS_), b))
                P.op("dve", lambda e: e.tensor_tensor(out=XT[:, CS_:NCH, c0:c1], in0=X32T[:, CS_:NCH, c0:c1], in1=X32T[:, CS_:NCH, c0:c1], op=ALU.mult),
                     reads=keys("X32", range(CS_, NCH), b), writes=keys("XT", range(CS_, NCH), b))
            bvs = []
            for (c0, c1, n, b) in info:
                bv = next_bank()
                bvs.append(bv)

                def mm_var(e, bv=bv, c0=c0, c1=c1, n=n):
                    ins = None
                    for k in range(NCH):
                        ins = e.matmul(PS[bv][:, 0:n], lhsT=ONESM[:], rhs=XT[:, k, c0:c1], start=(k == 0), stop=(k == NCH - 1))
                    return ins
                P.op("pe", mm_var, reads=keys("XT", allch, b) + ["ONESM"], writes=[("PS", bv)])
            for bi, ((c0, c1, n, b), bv) in enumerate(zip(info, bvs)):
                R = RR[bi % 2]
                rk = ("RR", bi % 2)
                P.op("act", lambda e: e.activation(out=R[:, 0:n], in_=PS[bv][:, 0:n], func=AF.Sqrt, bias=EPS[:], scale=1.0),
                     reads=[("PS", bv), "EPS"], writes=[rk])
                P.op("dve", lambda e: e.reciprocal(out=R[:, 0:n], in_=R[:, 0:n]), writes=[rk])
                P.op("dve", lambda e: e.tensor_tensor(out=X32T[:, :, c0:c1], in0=X32T[:, :, c0:c1],
                                                      in1=bcast(R[:, 0:n], [128, NCH, n], 1), op=ALU.mult),
                     reads=[rk], writes=keys("X32", allch, b))
                for c in range(NCH):
                    if c < 10:
                        P.op("dve", lambda e, c=c: e.tensor_scalar(out=X32T[:, c, c0:c1], in0=X32T[:, c, c0:c1],
                                                                   scalar1=GAM[:, l, c:c + 1], scalar2=BET[:, l, c:c + 1],
                                                                   op0=ALU.mult, op1=ALU.add),
                             reads=["GAM", "BET"], writes=[("X32", c, b)])
                    else:
                        P.op("act", lambda e, c=c: e.activation(out=X32T[:, c, c0:c1], in_=X32T[:, c, c0:c1], func=AF.Identity,
                                                                scale=GAM[:, l, c:c + 1], bias=BET[:, l, c:c + 1]),
                             reads=["GAM", "BET"], writes=[("X32", c, b)])
                P.op("dve", lambda e: e.tensor_copy(out=XT[:, 0:6, c0:c1], in_=X32T[:, 0:6, c0:c1]),
                     reads=keys("X32", range(6), b), writes=keys("XT", range(6), b))
                P.op("act", lambda e: e.copy(out=XT[:, 6:NCH, c0:c1], in_=X32T[:, 6:NCH, c0:c1]),
                     reads=keys("X32", range(6, NCH), b), writes=keys("XT", range(6, NCH), b))

        def wout_phase(l, blocks):
            for j in range(8):
                wv, wkey = wnext()
                for fsub in range(2):
                    c = 2 * j + fsub

                    def evac(bank, c0, c1, b, c=c):
                        n = c1 - c0
                        P.op("dve", lambda e: e.scalar_tensor_tensor(out=X32T[:, c, c0:c1], in0=X32T[:, c, c0:c1], scalar=ALPHA,
                                                                     in1=PS[bank][:, 0:n], op0=ALU.mult, op1=ALU.add),
                             reads=[("PS", bank)], writes=[("X32", c, b)])
                        P.op("act", lambda e: e.copy(out=XT[:, c, c0:c1], in_=X32T[:, c, c0:c1]), reads=[("X32", c, b)], writes=[("XT", c, b)])
                    modeF_chunk(wv, wkey, fsub, NCH, lambda k, c0, c1: HT[:, k, c0:c1], lambda b: keys("HT", allch, b), blocks, evac)
            layer_norm(l, blocks)

        def gate_phase(blocks, stagger):
            def piece(wv, wkey, j, bl):
                for fsub in range(2):
                    c = 2 * j + fsub

                    def evac(bank, c0, c1, b, c=c):
                        n = c1 - c0
                        P.op("act", lambda e: e.activation(out=HT[:, c, c0:c1], in_=PS[bank][:, 0:n], func=AF.Silu),
                             reads=[("PS", bank)], writes=[("HT", c, b)])
                    modeF_chunk(wv, wkey, fsub, NCH, lambda k, c0, c1: XT[:, k, c0:c1], lambda b: keys("XT", allch, b), bl, evac)
            j0 = 0
            if stagger:
                ws = [wnext(), wnext(), wnext(prefetch=False)]
                for j in range(3):
                    piece(ws[j][0], ws[j][1], j, blocks[:-1])
                for j in range(3):
                    piece(ws[j][0], ws[j][1], j, blocks[-1:])
                    if j == 0:
                        wprefetch()
                j0 = 3
            for j in range(j0, 8):
                wv, wkey = wnext()
                piece(wv, wkey, j, blocks)

        def dump(idx):
            if debug:
                P.barrier()
                P.op("sp", lambda e: e.dma_start(out=o_dbg[idx, :, :, :], in_=X32T[:]), dma=True)
                P.barrier()

        def ckpt(name):
            return stage == name

        def phases():
            with ExitStack() as s0:
                XIN = [sb("XIN%d" % i, [128, D], F32, s0) for i in range(2)]
                rt = 0
                r0 = 0
                while r0 < NT:
                    r1 = min(r0 + 128, NT)
                    n = r1 - r0
                    xi = XIN[rt % 2]
                    xk = ("XIN", rt % 2)
                    P.op("sp", lambda e, xi=xi, r0=r0, r1=r1, n=n: e.dma_start(out=xi[0:n, :], in_=xin[r0:r1, :]), writes=[xk], dma=True)
                    for cg in range(4):
                        bank = next_bank()

                        def tr(e, xi=xi, n=n, cg=cg, bank=bank):
                            ins = None
                            for cc in range(4):
                                c = cg * 4 + cc
                                ins = e.transpose(out=PS[bank][:, cc * 128:cc * 128 + n], in_=xi[0:n, c * 128:(c + 1) * 128], identity=IDF[0:n, 0:n])
                            return ins
                        P.op("pe", tr, reads=[xk, "IDF"], writes=[("PS", bank)])
                        src = PS[bank][:].rearrange("p (c t) -> p c t", c=4)[:, :, 0:n]
                        P.op("dve", lambda e, src=src, cg=cg, r0=r0, r1=r1: e.tensor_copy(out=X32T[:, cg * 4:cg * 4 + 4, r0:r1], in_=src),
                             reads=[("PS", bank)], writes=[("X32p0", cg, rt)])
                        P.op("act", lambda e, cg=cg, r0=r0, r1=r1: e.copy(out=XT[:, cg * 4:cg * 4 + 4, r0:r1], in_=X32T[:, cg * 4:cg * 4 + 4, r0:r1]),
                             reads=[("X32p0", cg, rt)], writes=[("XTp0", cg, rt)])
                    rt += 1
                    r0 = r1
                P.barrier()
                if ckpt('p0'):
                    return

            o_toks = []
            with ExitStack() as sa:
                US = [sb("U%d" % i, [128, NT], F32, sa) for i in range(2)]
                TA = sb("TA", [128, NT], F32, sa)
                TB = sb("TB", [128, NT], F32, sa)
                DT = sb("DT", [128, 4, NT], BF16, sa)
                STT = sb("STT", [60, 512], F32, sa)
                UKS = sb("UKS", [128, NCH, 20], F32, sa)
                FX = sb("FX", [128, 16], F32, sa)
                PST = sb("PST", [20, 1024], F32, sa)
                P.op("pool", lambda e: e.memset(DT[:], 0.0), writes=keys("DT", range(4), 0) + keys("DT", range(4), 1) + keys("DT", range(4), 2))
                for l in range(2):
                    blocks = BLK01
                    gate_phase(blocks, l >= 1)
                    if ckpt('gate%d' % l):
                        return
                    for g in range(4):
                        w = WINS[g]
                        P.op("sp", lambda e, g=g: e.dma_start(out=STT[:, :], in_=stp[l, :, 512 * g:512 * (g + 1)]), writes=["STT"], dma=True)
                        for i in range(2):
                            wv, wkey = wnext()
                            for fsub in range(2):
                                cc = 2 * i + fsub
                                c = 4 * g + cc
                                U = US[c % 2]
                                up = c % 2

                                def evac(bank, c0, c1, b):
                                    n = c1 - c0
                                    if b == 0:
                                        P.op("act", lambda e: e.copy(out=U[:, 0:4], in_=PS[bank][:, 0:4]), reads=[("PS", bank)], writes=[("U", up, 0)])
                                        P.op("act", lambda e: e.activation(out=U[:, 4:164], in_=PS[bank][:, 4:164], func=AF.Identity, scale=HM[:]),
                                             reads=[("PS", bank), "HM"], writes=[("U", up, 0)])
                                    else:
                                        P.op("act", lambda e: e.copy(out=U[:, c0:c1], in_=PS[bank][:, 0:n]), reads=[("PS", bank)], writes=[("U", up, b)])
                                modeF_chunk(wv, wkey, fsub, NCH, lambda k, c0, c1: XT[:, k, c0:c1], lambda b: keys("XT", allch, b), blocks, evac)
                                ukeys = [("U", up, 0), ("U", up, 1), ("U", up, 2)]
                                P.op("pool", lambda e, c=c: e.tensor_copy(out=UKS[:, c, 0:4], in_=U[:, 0:4]), reads=ukeys, writes=[("UKS", c)])
                                P.op("pool", lambda e, c=c: e.tensor_copy(out=UKS[:, c, 4:20], in_=U[:, NT - 16:NT]), reads=ukeys, writes=[("UKS", c)])
                                bp = next_bank()
                                P.op("pe", lambda e, bp=bp, cc=cc, g=g: e.matmul(PS[bp][:, 0:4], lhsT=STT[0:60, cc * 128:(cc + 1) * 128], rhs=PW[0:60, g, :],
                                                                               start=True, stop=True),
                                     reads=["STT", "PW"], writes=[("PS", bp)])
                                P.op("dve", lambda e: e.tensor_tensor(out=TA[:, 5:NT], in0=U[:, 5:NT], in1=U[:, 4:NT - 1], op=ALU.add), reads=ukeys, writes=["TA"])
                                Sb, Sk = TA, "TA"
                                if w >= 4:
                                    P.op("dve", lambda e: e.tensor_tensor(out=TB[:, 7:NT], in0=TA[:, 7:NT], in1=TA[:, 5:NT - 2], op=ALU.add), reads=["TA"], writes=["TB"])
                                    Sb, Sk = TB, "TB"
                                if w >= 8:
                                    P.op("dve", lambda e: e.tensor_tensor(out=TA[:, 11:NT], in0=TB[:, 11:NT], in1=TB[:, 7:NT - 4], op=ALU.add), reads=["TB"], writes=["TA"])
                                    Sb, Sk = TA, "TA"
                                if w >= 16:
                                    P.op("dve", lambda e: e.tensor_tensor(out=TB[:, 19:NT], in0=TA[:, 19:NT], in1=TA[:, 11:NT - 8], op=ALU.add), reads=["TA"], writes=["TB"])
                                    Sb, Sk = TB, "TB"
                                dk = keys("DT", [cc], 0) + keys("DT", [cc], 1) + keys("DT", [cc], 2)
                                P.op("dve", lambda e, Sb=Sb, cc=cc, w=w: e.scalar_tensor_tensor(out=DT[:, cc, 19:NT], in0=Sb[:, 19:NT], scalar=1.0 / w,
                                                                                         in1=U[:, 19:NT], op0=ALU.mult, op1=ALU.subtract),
                                     reads=[Sk] + ukeys, writes=dk)
                                P.op("dve", lambda e, Sb=Sb, g=g: e.tensor_tensor(out=FX[:, 0:15], in0=Sb[:, M0:M0 + 15], in1=RFIX[:, g, :], op=ALU.mult),
                                     reads=[Sk, "RFIX"], writes=["FX"])
                                P.op("dve", lambda e, cc=cc: e.tensor_tensor(out=DT[:, cc, M0:M0 + 15], in0=FX[:, 0:15], in1=U[:, M0:M0 + 15], op=ALU.subtract),
                                     reads=["FX"] + ukeys, writes=dk)
                                P.op("dve", lambda e, cc=cc, w=w, bp=bp: e.scalar_tensor_tensor(out=DT[:, cc, 0:4], in0=U[:, 0:4], scalar=(1.0 / w - 1.0),
                                                                                         in1=PS[bp][:, 0:4], op0=ALU.mult, op1=ALU.add),
                                     reads=[("PS", bp)] + ukeys, writes=dk)
                        wv, wkey = wnext()
                        for ee in range(4):
                            c = 4 * g + ee

                            def evac(bank, c0, c1, b, c=c):
                                n = c1 - c0
                                P.op("dve", lambda e: e.scalar_tensor_tensor(out=HT[:, c, c0:c1], in0=PS[bank][:, 0:n], scalar=SCA[:, l, c:c + 1],
                                                                             in1=HT[:, c, c0:c1], op0=ALU.mult, op1=ALU.mult),
                                     reads=[("PS", bank), "SCA"], writes=[("HT", c, b)])
                            modeF_chunk(wv, wkey, ee, 4, lambda k, c0, c1: DT[:, k, c0:c1], lambda b: keys("DT", range(4), b), blocks, evac)
                    if ckpt('grp%d' % l):
                        return
                    for hf in range(2):
                        for q2 in range(2):
                            q4 = hf * 2 + q2
                            bank = next_bank()

                            def tr(e, q4=q4, bank=bank):
                                ins = None
                                for cc in range(4):
                                    c = q4 * 4 + cc
                                    ins = e.transpose(out=PS[bank][0:20, cc * 128:(cc + 1) * 128], in_=UKS[:, c, :], identity=IDF[:])
                                return ins
                            P.op("pe", tr, reads=[("UKS", q4 * 4 + i) for i in range(4)] + ["IDF"], writes=[("PS", bank)])
                            P.op("act", lambda e, q2=q2, bank=bank: e.copy(out=PST[0:20, q2 * 512:(q2 + 1) * 512], in_=PS[bank][0:20, :]),
                                 reads=[("PS", bank)], writes=[("PST", q2)])
                        pk = [("PST", 0), ("PST", 1)]
                        o_toks.append(P.op("sp", lambda e, hf=hf: e.dma_start(out=o_npp[l, :, hf * 1024:(hf + 1) * 1024], in_=PST[5:20, :]), reads=pk, dma=True))
                        for s in range(4):
                            o_toks.append(P.op("sp", lambda e, s=s, hf=hf: e.dma_start(out=o_nps[l, s, 14:15, hf * 1024:(hf + 1) * 1024], in_=PST[s:s + 1, :]), reads=pk, dma=True))
                    for s in range(4):
                        o_toks.append(P.op("sp", lambda e, s=s: e.dma_start(out=o_nps[l, s, 0:14, :], in_=stp[l, 15 * s + 1:15 * s + 15, :]), dma=True))
                    if ckpt('np%d' % l):
                        return
                    wout_phase(l, blocks)
                    dump(l)
                    if ckpt('L%d' % l):
                        return
                P.barrier()

            with ExitStack() as sbk:
                KTP = [sb("KTP%d" % i, [128, 4, 9 * 128], BF16, sbk) for i in range(2)]
                P.op("pool", lambda e: e.memset(KTP[0][:], 0.0), writes=[("KT", i) for i in range(9)])
                P.op("pool", lambda e: e.memset(KTP[1][:], 0.0), writes=[("KT", i) for i in range(9)])
                VB = sb("VB", [128, 9, 256], BF16, sbk)
                KTS = sb("KTS", [128, 4, 4, 128], BF16, sbk)
                VSB = sb("VSB", [128, 4, 256], BF16, sbk)
                RTS = [sb("RT%d" % i, [128, 4, 4, 8], F32, sbk) for i in range(2)]
                ktiles = [(H0, H0 + 128)] + [(M0 + 128 * i, M0 + 128 * (i + 1)) for i in range(NMT)]
                tiles_kv = [(S0, S0 + 4, 0, None)] + [(a, b_, i + 1, i) for i, (a, b_) in enumerate(ktiles)]

                def rope(src3, dst3, n, csidx, nh, sk, dk, rs=0):
                    RT = RTS[rs]
                    r0k, r1k, r2k, r3k = ("RT0", rs), ("RT1", rs), ("RT2", rs), ("RT3", rs)
                    cos4 = CS[0:n, csidx, 0:8].unsqueeze(1).unsqueeze(1).broadcast_to([n, nh, 2, 8])
                    sin4 = CS[0:n, csidx, 8:16].unsqueeze(1).unsqueeze(1).broadcast_to([n, nh, 2, 8])
                    RTv = RT[:].rearrange("p a b d -> p (a b d)").rearrange("p (cs h two d) -> p cs h two d", cs=2, h=4, two=2)
                    xr = src3[:, :, 0:16].rearrange("p h (two d) -> p h two d", two=2)
                    P.op("act", lambda e: e.copy(out=dst3[:, :, 16:64], in_=src3[:, :, 16:64]), reads=[sk], writes=[dk])
                    P.op("dve", lambda e: e.tensor_tensor(out=RTv[0:n, 0, 0:nh, :, :], in0=xr, in1=cos4, op=ALU.mult), reads=[sk, "CS"], writes=[r0k])
                    P.op("dve", lambda e: e.tensor_tensor(out=RTv[0:n, 1, 0:nh, :, :], in0=xr, in1=sin4, op=ALU.mult), reads=[sk, "CS"], writes=[r1k])
                    P.op("dve", lambda e: e.tensor_tensor(out=dst3[:, :, 0:8], in0=RTv[0:n, 0, 0:nh, 0, :], in1=RTv[0:n, 1, 0:nh, 1, :], op=ALU.subtract),
                         reads=[r0k, r1k], writes=[dk])
                    P.op("dve", lambda e: e.tensor_tensor(out=dst3[:, :, 8:16], in0=RTv[0:n, 0, 0:nh, 1, :], in1=RTv[0:n, 1, 0:nh, 0, :], op=ALU.add),
                         reads=[r0k, r1k], writes=[dk])

                with ExitStack() as skv:
                    HTF = HT[:, 0:4, :].rearrange("p c t -> p (c t)").bitcast(F32)
                    KS = HTF[:, 0:1024].rearrange("p (s f) -> p s f", s=4)
                    VS = HTF[:, 1024:2048].rearrange("p (s f) -> p s f", s=4)
                    KRS = [sb("KR%d" % i, [128, 256], F32, skv) for i in range(2)]
                    KBDS = [sb("KBD%d" % i, [128, 4, 128], BF16, skv) for i in range(2)]
                    VR = sb("VR", [128, 256], F32, skv)
                    for s in range(4):
                        P.op("sp", lambda e, s=s: e.dma_start(out=KS[0:127, s, :], in_=ck[s, 1:128, :]), writes=[("KS", s)], dma=True)
                        P.op("sp", lambda e, s=s: e.dma_start(out=VS[0:127, s, :], in_=cv[s, 1:128, :]), writes=[("VS", s)], dma=True)
                    wv, wkey = wnext()
                    for (a, b_, csidx, kt) in tiles_kv:
                        n = b_ - a
                        bk = blk_of(a)
                        bank = next_bank()

                        def mmv(e, a=a, b_=b_, n=n, bank=bank):
                            ins = None
                            for k in range(NCH):
                                ins = e.matmul(PS[bank][0:n, 0:256], lhsT=XT[:, k, a:b_], rhs=wv[:, k, :], start=(k == 0), stop=(k == NCH - 1))
                            return ins
                        P.op("pe", mmv, reads=[wkey] + keys("XT", allch, bk), writes=[("PS", bank)])
                        if kt is None:
                            P.op("act", lambda e, bank=bank: e.copy(out=VR[0:4, :], in_=PS[bank][0:4, 0:256]), reads=[("PS", bank)], writes=["VR"])
                            for s in range(4):
                                P.op("sp", lambda e, s=s: e.dma_start(out=VS[127:128, s, :], in_=VR[s:s + 1, :]), reads=["VR"], writes=[("VS", s)], dma=True)
                        else:
                            P.op("act", lambda e, bank=bank, kt=kt: e.copy(out=VB[:, kt, :], in_=PS[bank][:, 0:256]), reads=[("PS", bank)], writes=[("VB", kt)])
                            if kt == 8:
                                P.op("dve", lambda e, bank=bank: e.tensor_copy(out=VR[:, :], in_=PS[bank][:, 0:256]), reads=[("PS", bank)], writes=["VR"])
                                o_toks.append(P.op("sp", lambda e: e.dma_start(out=o_nvp[:, :], in_=VR[:, :]), reads=["VR"], dma=True))
                    for s in range(4):
                        o_toks.append(P.op("sp", lambda e, s=s: e.dma_start(out=o_nvs[s, :, :], in_=VS[:, s, :]), reads=[("VS", s)], dma=True))
                        P.op("pool", lambda e, s=s: e.tensor_copy(out=VSB[:, s, :], in_=VS[:, s, :]), reads=[("VS", s)], writes=[("VSB", s)])
                    wv, wkey = wnext()

                    def kmm(tix):
                        (a, b_, csidx, kt) = tiles_kv[tix]
                        KR = KRS[tix % 2]
                        krk = ("KR", tix % 2)
                        n = b_ - a
                        bk = blk_of(a)
                        bank = next_bank()

                        def mmk(e):
                            ins = None
                            for k in range(NCH):
                                ins = e.matmul(PS[bank][0:n, 0:256], lhsT=XT[:, k, a:b_], rhs=wv[:, k, :], start=(k == 0), stop=(k == NCH - 1))
                            return ins
                        P.op("pe", mmk, reads=[wkey] + keys("XT", allch, bk), writes=[("PS", bank)])
                        src3 = PS[bank][0:n, 0:256].rearrange("p (h d) -> p h d", h=4)
                        dst3 = KR[0:n, :].rearrange("p (h d) -> p h d", h=4)
                        rope(src3, dst3, n, csidx, 4, ("PS", bank), krk, tix % 2)
                        if kt is None:
                            for s in range(4):
                                P.op("sp", lambda e, s=s: e.dma_start(out=KS[127:128, s, :], in_=KR[s:s + 1, :]), reads=[krk], writes=[("KS", s)], dma=True)
                        else:
                            if kt == 8:
                                o_toks.append(P.op("sp", lambda e: e.dma_start(out=o_nkp[:, :], in_=KR[:, :]), reads=[krk], dma=True))
                            KBD = KBDS[tix % 2]
                            KBD4 = KBD[:].rearrange("p g (a d) -> p g a d", a=2)
                            P.op("pool", lambda e: e.tensor_copy(out=KBD4, in_=bcast(dst3, [128, 4, 2, 64], 2)), reads=[krk], writes=[("KBD", tix % 2)])

                    def ktr(tix):
                        (a, b_, csidx, kt) = tiles_kv[tix]
                        if kt is None:
                            return
                        KBD = KBDS[tix % 2]
                        bt = next_bank()
                        pbt = PS[bt][:].bitcast(BF16)

                        def trk(e):
                            ins = None
                            for g in range(4):
                                ins = e.transpose(out=pbt[:, g * 128:(g + 1) * 128], in_=KBD[:, g, :], identity=IDB[:])
                            return ins
                        P.op("pe", trk, reads=[("KBD", tix % 2), "IDB"], writes=[("PS", bt)])
                        pv4 = pbt[:, 0:512].rearrange("p (g t) -> p g t", g=4)
                        P.op("act", lambda e: e.copy(out=KTP[0][0:64, :, kt * 128:(kt + 1) * 128], in_=pv4[0:64]), reads=[("PS", bt)], writes=[("KT", kt)])
                        P.op("act", lambda e: e.copy(out=KTP[1][64:128, :, kt * 128:(kt + 1) * 128], in_=pv4[64:128]), reads=[("PS", bt)], writes=[("KT", kt)])
                    kmm(0)
                    for tix in range(len(tiles_kv)):
                        if tix + 1 < len(tiles_kv):
                            kmm(tix + 1)
                        ktr(tix)
                    for s in range(4):
                        KBD = KBDS[s % 2]
                        KBD4 = KBD[:].rearrange("p g (a d) -> p g a d", a=2)
                        kbk = ("KBD", s % 2)
                        o_toks.append(P.op("sp", lambda e, s=s: e.dma_start(out=o_nks[s, :, :], in_=KS[:, s, :]), reads=[("KS", s)], dma=True))
                        src4 = KS[:, s, :].rearrange("p (h d) -> p h d", h=4)
                        P.op("pool", lambda e, src4=src4: e.tensor_copy(out=KBD4, in_=bcast(src4, [128, 4, 2, 64], 2)), reads=[("KS", s)], writes=[kbk])
                        bt = next_bank()
                        pbt = PS[bt][:].bitcast(BF16)

                        def trks(e, pbt=pbt):
                            ins = None
                            for g in range(4):
                                ins = e.transpose(out=pbt[:, g * 128:(g + 1) * 128], in_=KBD[:, g, :], identity=IDB[:])
                            return ins
                        P.op("pe", trks, reads=[kbk, "IDB"], writes=[("PS", bt)])
                        P.op("act", lambda e, pbt=pbt, s=s: e.copy(out=KTS[:, s, :, :], in_=pbt[:, 0:512].rearrange("p (g t) -> p g t", g=4)),
                             reads=[("PS", bt)], writes=[("KTS", s)])
                    P.barrier()

                if ckpt('kv'):

                    return
                with ExitStack() as sbl:
                    QT2 = [sb("QT2_%d" % i, [128, 4, 128], BF16, sbl) for i in range(2)]
                    QB = [sb("QB%d" % i, [128, 256], BF16, sbl) for i in range(2)]
                    ET = [HT[:, 4 * i:4 * i + 4, H0:H0 + 128] for i in range(4)] + [XT[:, 4 * i:4 * i + 4, H0:H0 + 128] for i in range(4)]
                    T1 = X32T[:, 0:4, H0:H0 + 128]
                    ESK = sb("ESK", [128, 32], F32, sbl)
                    ESK2 = sb("ESK2", [128, 4, 4], F32, sbl)
                    qtiles = [(S0, S0 + 4, 0)] + [(M0 + 128 * i, M0 + 128 * (i + 1), i + 2) for i in range(NMT)]
                    for j in range(2):
                        l = 2 + j
                        blocks = BLK23
                        P.op("sp", lambda e: e.dma_start(out=ESK[:], in_=sinks_b[j, :].partition_broadcast(128)), writes=["ESK"], dma=True)
                        P.op("act", lambda e: e.activation(out=ESK[:], in_=ESK[:], func=AF.Exp), writes=["ESK"])
                        ev = ESK[:].rearrange("p (g i two) -> p g i two", g=4, two=2)
                        P.op("dve", lambda e: e.tensor_copy(out=ESK2[0:64, :, :], in_=ev[0:64, :, :, 0]), reads=["ESK"], writes=["ESK2"])
                        P.op("dve", lambda e: e.tensor_copy(out=ESK2[64:128, :, :], in_=ev[64:128, :, :, 1]), reads=["ESK"], writes=["ESK2"])
                        gate_phase(blocks, True)
                        if ckpt('bgate%d' % l):
                            return
                        NTL = len(qtiles)
                        wqs = {}
                        qtr_done = set()

                        def qmm(gt):
                            g, ti = divmod(gt, NTL)
                            if g not in wqs:
                                wqs[g] = [wnext(), wnext()]
                            wq = wqs[g]
                            (a, b_, csidx) = qtiles[ti]
                            n = b_ - a
                            bk = blk_of(a)
                            for i in range(2):
                                wv, wkey = wq[i]
                                bank = next_bank()

                                def mmq(e, bank=bank, wv=wv):
                                    ins = None
                                    for k in range(NCH):
                                        ins = e.matmul(PS[bank][0:n, 0:256], lhsT=XT[:, k, a:b_], rhs=wv[:, k, :], start=(k == 0), stop=(k == NCH - 1))
                                    return ins
                                P.op("pe", mmq, reads=[wkey] + keys("XT", allch, bk), writes=[("PS", bank)])
                                qb = QB[i]
                                qbk = ("QB", i)
                                src3 = PS[bank][0:n, 0:256].rearrange("p (h d) -> p h d", h=4)
                                dst3 = qb[0:n, :].rearrange("p (h d) -> p h d", h=4)
                                rope(src3, dst3, n, csidx, 4, ("PS", bank), qbk, i)

                        def qtr(gt):
                            g, ti = divmod(gt, NTL)
                            (a, b_, csidx) = qtiles[ti]
                            n = b_ - a
                            qt = QT2[gt % 2]
                            for i in range(2):
                                qb = QB[i]
                                qbk = ("QB", i)
                                bt = next_bank()
                                pbt = PS[bt][:].bitcast(BF16)

                                def trq(e, pbt=pbt, qb=qb):
                                    ins = None
                                    for cc in range(2):
                                        ins = e.transpose(out=pbt[:, cc * 128:cc * 128 + n], in_=qb[0:n, 128 * cc:128 * (cc + 1)], identity=IDB[0:n, 0:n])
                                    return ins
                                P.op("pe", trq, reads=[qbk, "IDB"], writes=[("PS", bt)])
                                P.op("act", lambda e, pbt=pbt, i=i: e.copy(out=qt[:, 2 * i:2 * i + 2, 0:n],
                                                                          in_=pbt[:, 0:256].rearrange("p (c t) -> p c t", c=2)[:, :, 0:n]),
                                     reads=[("PS", bt)], writes=[("QT2", gt % 2, i)])
                            qtr_done.add(gt)

                        units = []
                        for g in range(4):
                            units.append((g * NTL, 0, 4, [(KTS[:, s_, g, :], VSB[:, s_, 64 * g:64 * (g + 1)], "S", ("KTS", s_), ("VSB", s_)) for s_ in range(4)], S0, 0, g))
                            for qb_ in range(NMT):
                                kl = []
                                for kt, mi in ((qb_, 2 if qb_ == 0 else 0), (qb_ + 1, 1)):
                                    kl.append(((KTP[0][:, g, kt * 128:(kt + 1) * 128], KTP[1][:, g, kt * 128:(kt + 1) * 128]), VB[:, kt, 64 * g:64 * (g + 1)], mi, ("KT", kt), ("VB", kt)))
                                units.append((g * NTL + qb_ + 1, 0, 128, kl, M0 + 128 * qb_, 1 if qb_ < 4 else 2, g))

                        def stageA(ui):
                            (gt, off, nq, kl, ha, hb, g) = units[ui]
                            assert gt in qtr_done, (gt, ui)
                            N = 4 * nq
                            qt = QT2[gt % 2]
                            qkeys = [("QT2", gt % 2, 0), ("QT2", gt % 2, 1)]
                            banks = []
                            ets = []
                            if kl[0][2] == "S":
                                bks = [next_bank(), next_bank()]

                                def mms_s(e):
                                    ins = None
                                    for s_, (kTap, vap, mi, kkey, vkey) in enumerate(kl):
                                        for par in range(2):
                                            p0, p1 = 64 * par, 64 * par + 64
                                            ins = e.matmul(PS[bks[par]][:, 0:16].rearrange("p (c s) -> p c s", s=4)[:, :, s_],
                                                           lhsT=kTap[p0:p1, :], rhs=qt[p0:p1, :, s_:s_ + 1], start=True, stop=True)
                                    return ins
                                P.op("pe", mms_s, reads=[x[3] for x in kl] + qkeys, writes=[("PS", bks[0]), ("PS", bks[1])])
                                for par in range(2):
                                    eti = (ui % 2) * 4 + par
                                    et = ET[eti]
                                    ek = ("ET", eti)
                                    P.op("act", lambda e, et=et, par=par: e.activation(out=et[:, :, 0:4], in_=PS[bks[par]][:, 0:16].rearrange("p (c t) -> p c t", c=4),
                                                                                   func=AF.Exp, scale=0.125),
                                         reads=[("PS", bks[par])], writes=[ek])
                                    ets.append((et, ek, None, None, par))
                                return ets
                            for ki, (kTap, vap, mi, kkey, vkey) in enumerate(kl):
                                for par in range(2):
                                    banks.append((next_bank(), ki, par, kTap, mi))

                            def mms(e):
                                ins = None
                                for (bank, ki, par, kTap, mi) in banks:
                                    p0, p1 = 64 * par, 64 * par + 64
                                    if isinstance(kTap, tuple):
                                        ins = e.matmul(PS[bank][:, 0:N], lhsT=kTap[par], rhs=qt[:, :, off:off + nq], start=True, stop=(mi is None))
                                    else:
                                        ins = e.matmul(PS[bank][:, 0:N], lhsT=kTap[p0:p1, :], rhs=qt[p0:p1, :, off:off + nq], start=True, stop=(mi is None))
                                for (bank, ki, par, kTap, mi) in banks:
                                    if mi is not None:
                                        ins = e.matmul(PS[bank][:, 0:N], lhsT=IDB[:], rhs=MASK[:, mi, 0:N], start=False, stop=True)
                                return ins
                            P.op("pe", mms, reads=[x[3] for x in kl] + ["MASK", "IDB"] + qkeys, writes=[("PS", bk_[0]) for bk_ in banks])
                            for (bank, ki, par, kTap, mi) in banks:
                                eti = (ui % 2) * 4 + ki * 2 + par
                                et = ET[eti]
                                ek = ("ET", eti)
                                P.op("act", lambda e, et=et, bank=bank: e.activation(out=et[:, :, 0:nq], in_=PS[bank][:, 0:N].rearrange("p (c t) -> p c t", c=4),
                                                                                 func=AF.Exp, scale=0.125),
                                     reads=[("PS", bank)], writes=[ek])
                                ets.append((et, ek, kl[ki][1], kl[ki][4], par))
                            return ets

                        def stageB(ui, ets):
                            (gt, off, nq, kl, ha, hb, g) = units[ui]
                            N = 4 * nq
                            bo = next_bank()
                            bd = next_bank()

                            def mmo_s(e):
                                ins = None
                                for par in range(2):
                                    p0, p1 = 64 * par, 64 * par + 64
                                    et = [x for x in ets if x[4] == par][0][0]
                                    for s_, (kTap, vap, mi, kkey, vkey) in enumerate(kl):
                                        ins = e.matmul(PS[bo][p0:p1, 0:16].rearrange("p (c s) -> p c s", s=4)[:, :, s_], lhsT=vap, rhs=et[:, :, s_:s_ + 1], start=True, stop=True)
                                    for s_, (kTap, vap, mi, kkey, vkey) in enumerate(kl):
                                        ins = e.matmul(PS[bd][p0:p1, 0:16].rearrange("p (c s) -> p c s", s=4)[:, :, s_], lhsT=ONES64[:], rhs=et[:, :, s_:s_ + 1], start=True, stop=True)
                                return ins

                            def mmo(e):
                                ins = None
                                for par in range(2):
                                    p0, p1 = 64 * par, 64 * par + 64
                                    sel = [x for x in ets if x[4] == par]
                                    for ii, (et, ek, vap, vkey, _) in enumerate(sel):
                                        ins = e.matmul(PS[bo][p0:p1, 0:N], lhsT=vap, rhs=et[:, :, 0:nq], start=(ii == 0), stop=(ii == len(sel) - 1))
                                    for ii, (et, ek, vap, vkey, _) in enumerate(sel):
                                        ins = e.matmul(PS[bd][p0:p1, 0:N], lhsT=ONES64[:], rhs=et[:, :, 0:nq], start=(ii == 0), stop=(ii == len(sel) - 1))
                                return ins
                            if kl[0][2] == "S":
                                P.op("pe", mmo_s, reads=[x[1] for x in ets] + [x[4] for x in kl] + ["ONES64"], writes=[("PS", bo), ("PS", bd)])
                            else:
                                P.op("pe", mmo, reads=[x[1] for x in ets] + [x[3] for x in ets] + ["ONES64"], writes=[("PS", bo), ("PS", bd)])
                            R = RR[ui % 2]
                            rk = ("RR", ui % 2)
                            P.op("dve", lambda e: e.tensor_tensor(out=R[:, 0:N].rearrange("p (c t) -> p c t", c=4),
                                                                  in0=PS[bd][:, 0:N].rearrange("p (c t) -> p c t", c=4),
                                                                  in1=bcast(ESK2[:, g, :], [128, 4, nq], 2), op=ALU.add),
                                 reads=[("PS", bd), "ESK2"], writes=[rk])
                            P.op("act", lambda e: e.activation(out=R[:, 0:N], in_=R[:, 0:N], func=AF.Ln), writes=[rk])
                            P.op("act", lambda e: e.activation(out=R[:, 0:N], in_=R[:, 0:N], func=AF.Exp, scale=-1.0), writes=[rk])
                            P.op("dve", lambda e: e.tensor_tensor(out=T1[:, :, 0:nq], in0=PS[bo][:, 0:N].rearrange("p (c t) -> p c t", c=4),
                                                                  in1=R[:, 0:N].rearrange("p (c t) -> p c t", c=4), op=ALU.mult),
                                 reads=[("PS", bo), rk], writes=["T1"])
                            hk = keys("HT", range(4 * g, 4 * g + 4), hb)
                            P.op("dve", lambda e: e.tensor_tensor(out=HT[:, 4 * g:4 * g + 4, ha:ha + nq],
                                                                  in0=T1[:, :, 0:nq],
                                                                  in1=HT[:, 4 * g:4 * g + 4, ha:ha + nq], op=ALU.mult),
                                 reads=["T1"], writes=hk)

                        TT = 4 * NTL
                        last_unit = {}
                        for ui_, u_ in enumerate(units):
                            last_unit[u_[0]] = ui_
                        qmm(0)
                        qtr(0)
                        qmm(1)
                        qtr(1)
                        pend = stageA(0)
                        qmm(2)
                        pend_tr = 2
                        nextq = 3
                        for ui in range(len(units)):
                            if ui + 1 < len(units) and units[ui + 1][0] not in qtr_done:
                                assert pend_tr == units[ui + 1][0], (pend_tr, units[ui + 1][0])
                                qtr(pend_tr)
                                pend_tr = None
                            nxt = stageA(ui + 1) if ui + 1 < len(units) else None
                            if pend_tr is not None and last_unit[pend_tr - 2] <= ui + 1:
                                qtr(pend_tr)
                                pend_tr = None
                            stageB(ui, pend)
                            if pend_tr is None and nextq < TT:
                                qmm(nextq)
                                pend_tr = nextq
                                nextq += 1
                            pend = nxt
                        assert nextq == TT and pend_tr is None, (nextq, pend_tr)
                        if ckpt('attn%d' % l):
                            return
                        wout_phase(l, blocks)
                        dump(l)
                        if ckpt('L%d' % l):
                            return
                    P.barrier()

            with ExitStack() as so:
                OST = [sb("OST%d" % i, [128, D], F32, so) for i in range(2)]
                units = [(S0, S0 + 4, None)] + [(M0 + 128 * i, M0 + 128 * (i + 1), i) for i in range(NMT)]
                for ui, (a, b_, mt) in enumerate(units):
                    n = b_ - a
                    ost = OST[ui % 2]
                    okey = ("OST", ui % 2)
                    for cg in range(4):
                        bank = next_bank()

                        def tro(e, cg=cg, bank=bank, a=a, b_=b_, n=n):
                            ins = None
                            for cc in range(4):
                                c = cg * 4 + cc
                                ins = e.transpose(out=PS[bank][0:n, cc * 128:(cc + 1) * 128], in_=X32T[:, c, a:b_], identity=IDF[:])
                            return ins
                        P.op("pe", tro, reads=["IDF"], writes=[("PS", bank)])
                        eng = "dve" if cg % 2 == 0 else "act"
                        if eng == "dve":
                            P.op("dve", lambda e, cg=cg, bank=bank, n=n, ost=ost: e.tensor_copy(out=ost[0:n, cg * 512:(cg + 1) * 512], in_=PS[bank][0:n, :]),
                                 reads=[("PS", bank)], writes=[okey + (cg,)])
                        else:
                            P.op("act", lambda e, cg=cg, bank=bank, n=n, ost=ost: e.copy(out=ost[0:n, cg * 512:(cg + 1) * 512], in_=PS[bank][0:n, :]),
                                 reads=[("PS", bank)], writes=[okey + (cg,)])
                    rk = [okey + (cg,) for cg in range(4)]
                    if mt is None:
                        o_toks.append(P.op("sp", lambda e, ost=ost: e.dma_start(out=o_ys[:, :], in_=ost[0:4, :]), reads=rk, dma=True))
                    else:
                        o_toks.append(P.op("sp", lambda e, ost=ost, mt=mt: e.dma_start(out=o_yp[128 * mt:128 * (mt + 1), :], in_=ost[:, :]), reads=rk, dma=True))
        if stage != 'const':
            phases()
        if stage is not None:
            P.barrier()
            if debug:
                P.op('sp', lambda e: e.dma_start(out=o_dbg[3, :, :, :], in_=X32T[:]), dma=True)
        P.final_wait("sp")
    return nc


_NC_CACHE = {}


def host_consts(core):
    b, h = core // 2, core % 2
    start = 1024 * h
    inv = (np.float32(500000.0) ** (-np.arange(0, 16, 2, dtype=np.float32) / np.float32(16.0))).astype(np.float32)
    pos = np.zeros((128, 10), np.float32)
    pos[:, 0] = PAST
    for t in range(9):
        pos[:, t + 1] = start - 128 + 128 * t + np.arange(128)
    ang = (pos[:, :, None].astype(np.float32) * inv[None, None, :]).astype(np.float32)
    cs = np.concatenate([np.cos(ang), np.sin(ang)], axis=-1).astype(np.float32)
    k = np.arange(128)[:, None]
    q = np.arange(128)[None, :]
    mprev = (k > q).astype(np.float32)
    mown = (k <= q).astype(np.float32)
    mask = np.stack([mprev, mown, mprev * float(h)], axis=1)
    mask = np.tile((mask - 1.0) * 30000.0, (1, 1, 4)).astype(ml_dtypes.bfloat16)
    hm = np.full((128, 1), float(h), np.float32)
    pw = np.zeros((60, 4, 4), np.float32)
    for g, w in enumerate(WINS):
        for s in range(4):
            for jj in range(15 - (w - 1), 15):
                pw[15 * s + jj, g, s] = 1.0 / w
    rfix = np.zeros((128, 4, 15), np.float32)
    for g, w in enumerate(WINS):
        for t in range(15):
            cnt = min(w, t + 1) if h == 0 else w
            rfix[:, g, t] = 1.0 / cnt
    return cs, mask, hm, pw, rfix


def kernel(x_prompt, x_sample, state_pool, cache_k, cache_v, w_in_a, w_grp_a, scale_a, w_out_a,
           w_kv, w_in_b, sinks_b, w_out_b, ln_g, ln_b, _debug=False, _stage=None, _cores=None):
    f = lambda a: np.ascontiguousarray(np.asarray(a, dtype=np.float32))
    x_prompt, x_sample, state_pool, cache_k, cache_v = map(f, (x_prompt, x_sample, state_pool, cache_k, cache_v))
    shared = {"w_in_a": f(w_in_a), "w_grp_a": f(w_grp_a), "scale_a": f(scale_a), "w_out_a": f(w_out_a), "w_kv": f(w_kv),
              "w_in_b": f(w_in_b), "sinks_b": f(sinks_b), "w_out_b": f(w_out_b), "ln_g": f(ln_g), "ln_b": f(ln_b)}
    key = (bool(_debug), _stage)
    if key not in _NC_CACHE:
        _NC_CACHE[key] = build(debug=bool(_debug), stage=_stage)
    nc = _NC_CACHE[key]
    in_maps = []
    for core in range(8):
        b, h = core // 2, core % 2
        start = 1024 * h
        xin = np.zeros((NT, D), np.float32)
        xin[0:4] = x_sample[4 * core:4 * core + 4, 0]
        if h == 1:
            xin[E0:M0] = x_prompt[b, start - 160:start]
        xin[M0:] = x_prompt[b, start:start + 1024]
        cs, mask, hm, pw, rfix = host_consts(core)
        m = dict(shared)
        m.update({"xin": xin,
                  "stp": np.ascontiguousarray(state_pool[:, 4 * core:4 * core + 4].reshape(2, 60, D)),
                  "ck": np.ascontiguousarray(cache_k[4 * core:4 * core + 4].reshape(4, 128, 256)),
                  "cv": np.ascontiguousarray(cache_v[4 * core:4 * core + 4].reshape(4, 128, 256)),
                  "c_cs": cs, "c_mask": mask, "c_hm": hm, "c_pw": pw, "c_rfix": rfix})
        in_maps.append(m)
    if _cores is not None:
        res = run_bass_kernel_spmd(nc, [in_maps[c] for c in _cores], core_ids=list(range(len(_cores))))
        return res.results
    res = run_bass_kernel_spmd(nc, in_maps, core_ids=list(range(8)))
    R = res.results
    y_prompt = np.zeros((4, 2048, D), np.float32)
    y_sample = np.zeros((32, 1, D), np.float32)
    npp = np.zeros((2, 4, 15, D), np.float32)
    nps = np.zeros((2, 32, 15, D), np.float32)
    nkp = np.zeros((4, 128, 4, 64), np.float32)
    nvp = np.zeros((4, 128, 4, 64), np.float32)
    nks = np.zeros((32, 128, 4, 64), np.float32)
    nvs = np.zeros((32, 128, 4, 64), np.float32)
    for core in range(8):
        b, h = core // 2, core % 2
        r = R[core]
        y_prompt[b, 1024 * h:1024 * (h + 1)] = r["o_yp"]
        y_sample[4 * core:4 * core + 4, 0] = r["o_ys"]
        nps[:, 4 * core:4 * core + 4] = r["o_nps"]
        nks[4 * core:4 * core + 4] = r["o_nks"].reshape(4, 128, 4, 64)
        nvs[4 * core:4 * core + 4] = r["o_nvs"].reshape(4, 128, 4, 64)
        if h == 1:
            npp[:, b] = r["o_npp"]
            nkp[b] = r["o_nkp"].reshape(128, 4, 64)
            nvp[b] = r["o_nvp"].reshape(128, 4, 64)
    if _debug:
        return (y_prompt, y_sample, npp, nps, nkp, nvp, nks, nvs), [r["o_dbg"] for r in R]
    return (y_prompt, y_sample, npp, nps, nkp, nvp, nks, nvs)
```
